# Optimizing a Trainium2 kernel written in Bass

```python
import jax, jax.numpy as jnp
from jax import lax
import numpy as np

D_MODEL = 1024
BATCH = 8
SEQ = 4096
DEPTH = 1

GRID_W = 64
CTX_LEN = 256
FOURIER_WIDTH = D_MODEL // 2
FOURIER_GROUPS = 4
FOURIER_GROUP_DIM = FOURIER_WIDTH // FOURIER_GROUPS
LRU_WIDTH = D_MODEL
LRU_HEADS = 8
LRU_HEAD_DIM = LRU_WIDTH // LRU_HEADS
LRU_CONV_W = 4
LRU_C = 8.0
N_DIR = 2
D_FF = 2816
N_MOD = 6
RMS_EPS = 1e-6
IN_WIDTH = FOURIER_WIDTH + 2 * LRU_WIDTH + 2 * D_MODEL

kernel_name = "fourier_rglru_convffn_hybrid_dit"


def rms_norm(x, g):
    xf = x.astype(jnp.float32)
    y = xf * lax.rsqrt(jnp.mean(xf * xf, axis=-1, keepdims=True) + RMS_EPS)
    return (y * g.astype(jnp.float32)).astype(x.dtype)


def modulate(x, g, shift, scale):
    return rms_norm(x, g) * (1 + scale) + shift


def ada_params(cond, w, b):
    m = jax.nn.silu(cond) @ w + b
    return jnp.split(m[:, None, :], N_MOD, axis=-1)


def split_in(z):
    return jnp.split(z, [FOURIER_WIDTH, FOURIER_WIDTH + LRU_WIDTH, FOURIER_WIDTH + 2 * LRU_WIDTH], axis=-1)


def fourier_mix(u):
    b, n, _ = u.shape
    uf = u.astype(jnp.float32).reshape(b, n, FOURIER_GROUPS, FOURIER_GROUP_DIM)
    y = jnp.fft.fft2(uf, axes=(1, 3), norm="ortho").real
    return y.reshape(b, n, FOURIER_WIDTH).astype(u.dtype)


def dwconv1d(u, w, bias):
    c = u.shape[-1]
    left = LRU_CONV_W // 2
    y = lax.conv_general_dilated(u, w[:, None, :].astype(u.dtype), window_strides=(1,),
                                 padding=[(left, LRU_CONV_W - 1 - left)],
                                 dimension_numbers=("NWC", "WIO", "NWC"), feature_group_count=c)
    return y + bias


def dwconv2d(u, w, bias):
    c = u.shape[-1]
    y = lax.conv_general_dilated(u, w[:, :, None, :].astype(u.dtype), window_strides=(1, 1),
                                 padding="SAME", dimension_numbers=("NHWC", "HWIO", "NHWC"),
                                 feature_group_count=c)
    return y + bias


def lru_coeffs(u, ga_w, ga_b, gx_w, gx_b, lam):
    b, n, _ = u.shape
    uh = u.reshape(b, n, LRU_HEADS, LRU_HEAD_DIM)
    r = jax.nn.sigmoid((jnp.einsum("bnhi,hij->bnhj", uh, ga_w).reshape(b, n, LRU_WIDTH) + ga_b).astype(jnp.float32))
    i = jax.nn.sigmoid((jnp.einsum("bnhi,hij->bnhj", uh, gx_w).reshape(b, n, LRU_WIDTH) + gx_b).astype(jnp.float32))
    log_a = -LRU_C * r * jax.nn.softplus(-lam.astype(jnp.float32))
    a = jnp.exp(log_a)
    mult = jnp.sqrt(jnp.maximum(-jnp.expm1(2.0 * log_a), 1e-12))
    return a, mult * (i * u.astype(jnp.float32))


def _combine(e1, e2):
    a1, b1 = e1
    a2, b2 = e2
    return a1 * a2, a2 * b1 + b2


def linear_scan(a, b, h0):
    b = b.at[:, 0].add(a[:, 0] * h0)
    return lax.associative_scan(_combine, (a, b), axis=1)[1]


def lru_scans(u_x, conv_w, conv_b, ga_w, ga_b, gx_w, gx_b, lam, h0):
    u = dwconv1d(u_x, conv_w, conv_b)
    a_f, b_f = lru_coeffs(u, ga_w[0], ga_b[0], gx_w[0], gx_b[0], lam[0])
    a_b, b_b = lru_coeffs(u, ga_w[1], ga_b[1], gx_w[1], gx_b[1], lam[1])
    h_f = linear_scan(a_f, b_f, h0[0])
    h_b = jnp.flip(linear_scan(jnp.flip(a_b, 1), jnp.flip(b_b, 1), h0[1]), 1)
    return h_f, h_b


def merge_mixers(z_f, z_y, z_g, lru_h, w_f, w_r, w_o):
    f = fourier_mix(z_f) @ w_f
    r = (lru_h.astype(z_y.dtype) * jax.nn.gelu(z_y)) @ w_r
    g_f, g_r = jnp.split(jax.nn.sigmoid(z_g), 2, axis=-1)
    return (g_f * f + g_r * r) @ w_o


def conv_ffn(h, rows, w_up, conv_w, conv_b, w_down):
    b, n, _ = h.shape
    u = (h @ w_up).reshape(b, rows, n // rows, 2 * D_FF)
    u = dwconv2d(u, conv_w, conv_b).reshape(b, n, 2 * D_FF)
    gate, val = jnp.split(u, 2, axis=-1)
    return (jax.nn.gelu(gate) * val) @ w_down


def setup_inputs(seed: int = 0) -> dict:
    key = jax.random.key(seed)
    ks = jax.random.split(key, 24)
    f32 = jnp.float32
    L, D, R, F = DEPTH, D_MODEL, LRU_WIDTH, FOURIER_WIDTH

    def nrm(k, shape, scale):
        return jax.random.normal(k, shape, f32) * scale

    a_target = jax.random.uniform(ks[12], (L, N_DIR, R), f32, 0.9, 0.999)
    sig = a_target ** (1.0 / LRU_C)
    lru_lambda = jnp.log(sig) - jnp.log1p(-sig)
    return {
        "x": nrm(ks[0], (BATCH, SEQ, D), 1.0),
        "c": nrm(ks[1], (BATCH, D), 1.0),
        "ctx": nrm(ks[2], (BATCH, CTX_LEN, D), 1.0),
        "c_ctx": nrm(ks[3], (D,), 1.0),
        "mod_w": nrm(ks[4], (L, D, N_MOD * D), 0.5 * D ** -0.5),
        "mod_b": nrm(ks[5], (L, N_MOD * D), 0.02),
        "norm1_g": 1.0 + nrm(ks[6], (L, D), 0.02),
        "norm2_g": 1.0 + nrm(ks[7], (L, D), 0.02),
        "w_in": nrm(ks[8], (L, D, IN_WIDTH), D ** -0.5),
        "lru_conv_w": nrm(ks[9], (L, LRU_CONV_W, R), LRU_CONV_W ** -0.5),
        "lru_conv_b": nrm(ks[10], (L, R), 0.02),
        "lru_ga_w": nrm(ks[11], (L, N_DIR, LRU_HEADS, LRU_HEAD_DIM, LRU_HEAD_DIM), LRU_HEAD_DIM ** -0.5),
        "lru_ga_b": nrm(ks[13], (L, N_DIR, R), 0.02),
        "lru_gx_w": nrm(ks[14], (L, N_DIR, LRU_HEADS, LRU_HEAD_DIM, LRU_HEAD_DIM), LRU_HEAD_DIM ** -0.5),
        "lru_gx_b": nrm(ks[15], (L, N_DIR, R), 0.02),
        "lru_lambda": lru_lambda,
        "w_fourier": nrm(ks[16], (L, F, D), F ** -0.5),
        "w_lru_out": nrm(ks[17], (L, R, D), R ** -0.5),
        "w_o": nrm(ks[18], (L, D, D), D ** -0.5),
        "ffn_w_up": nrm(ks[19], (L, D, 2 * D_FF), D ** -0.5),
        "ffn_conv_w": nrm(ks[20], (L, 3, 3, 2 * D_FF), 1.0 / 3.0),
        "ffn_conv_b": nrm(ks[21], (L, 2 * D_FF), 0.02),
        "ffn_w_down": nrm(ks[22], (L, D_FF, D), D_FF ** -0.5),
        "final_g": 1.0 + nrm(ks[23], (D,), 0.02),
    }


def reference(x, c, ctx, c_ctx, mod_w, mod_b, norm1_g, norm2_g, w_in, lru_conv_w, lru_conv_b,
              lru_ga_w, lru_ga_b, lru_gx_w, lru_gx_b, lru_lambda, w_fourier, w_lru_out, w_o,
              ffn_w_up, ffn_conv_w, ffn_conv_b, ffn_w_down, final_g):
    bsz, n_lat, _ = x.shape
    rows = n_lat // GRID_W
    zero_state = jnp.zeros((N_DIR, bsz, LRU_WIDTH), jnp.float32)
    for l in range(DEPTH):
        last = l == DEPTH - 1
        sh1, sc1, g1, sh2, sc2, g2 = ada_params(c, mod_w[l], mod_b[l])
        csh1, csc1, cg1, csh2, csc2, cg2 = ada_params(c_ctx[None], mod_w[l], mod_b[l])
        lru_p = (lru_conv_w[l], lru_conv_b[l], lru_ga_w[l], lru_ga_b[l], lru_gx_w[l], lru_gx_b[l], lru_lambda[l])

        h_ctx = modulate(ctx, norm1_g[l], csh1, csc1)
        if last:
            zc_x = h_ctx @ w_in[l][:, FOURIER_WIDTH:FOURIER_WIDTH + LRU_WIDTH]
        else:
            zc_f, zc_x, zc_y, zc_g = split_in(h_ctx @ w_in[l])
        hcf, hcb = lru_scans(zc_x, *lru_p, zero_state)
        h0_lat = jnp.stack([hcf[:, -1], hcb[:, 0]])

        h_lat = modulate(x, norm1_g[l], sh1, sc1)
        z_f, z_x, z_y, z_g = split_in(h_lat @ w_in[l])
        hf, hb = lru_scans(z_x, *lru_p, h0_lat)
        x = x + g1 * merge_mixers(z_f, z_y, z_g, hf + hb, w_fourier[l], w_lru_out[l], w_o[l])
        x = x + g2 * conv_ffn(modulate(x, norm2_g[l], sh2, sc2), rows,
                              ffn_w_up[l], ffn_conv_w[l], ffn_conv_b[l], ffn_w_down[l])

        if not last:
            ctx = ctx + cg1 * merge_mixers(zc_f, zc_y, zc_g, hcf + hcb, w_fourier[l], w_lru_out[l], w_o[l])
            ctx = ctx + cg2 * conv_ffn(modulate(ctx, norm2_g[l], csh2, csc2), 1,
                                       ffn_w_up[l], ffn_conv_w[l], ffn_conv_b[l], ffn_w_down[l])
    return rms_norm(x, final_g)
```

```python
import numpy as np
import ml_dtypes
import concourse.bass as bass
import concourse.mybir as mybir
from concourse.bass_utils import run_bass_kernel_spmd

AF = mybir.ActivationFunctionType
ALU = mybir.AluOpType
F32 = mybir.dt.float32
BF16 = mybir.dt.bfloat16
SB_BASE = 16640
SB_END = 229344
NTOK = 4096
D = 1024
NCTX = 256
DFF = 2816
EPS = 1e-6


class Tk:
    __slots__ = ("key", "val")

    def __init__(self, key, val):
        self.key = key
        self.val = val


class Buf:
    __slots__ = ("name", "w", "r")

    def __init__(self, name):
        self.name = name
        self.w = None
        self.r = []


class Prog:
    ENGS = ("pe", "act", "dve", "pool", "sp")

    def __init__(self, nc):
        self.nc = nc
        self.q = {e: [] for e in self.ENGS}
        self.cnt = {e: 0 for e in self.ENGS}
        self.seen = {e: {} for e in self.ENGS}
        self.pend = {e: [] for e in self.ENGS}
        self.dma_keys = []
        self.sb_off = SB_BASE
        self.offs = {}

    def sb(self, name, shape, dtype, off=None):
        esz = 2 if dtype == BF16 else 4
        n = 1
        for s in shape[1:]:
            n *= s
        nbytes = (n * esz + 63) // 64 * 64
        if off is None:
            off = self.sb_off
            self.sb_off += nbytes
        assert off % 32 == 0 and off + nbytes <= SB_END, (name, off, nbytes)
        self.offs[name] = off
        return self.nc.alloc_sbuf_tensor_at(name, list(shape), dtype, offset=off)

    def _wait(self, eng, tk):
        if tk is None:
            return
        if eng == "pe" and tk.key == "pe":
            return
        if self.seen[eng].get(tk.key, 0) >= tk.val:
            return
        self.seen[eng][tk.key] = tk.val
        self.q[eng].append(("wait", tk.key, tk.val))

    def _deps(self, eng, reads, writes, extra):
        for b in reads:
            self._wait(eng, b.w)
        for b in writes:
            self._wait(eng, b.w)
            for t in b.r:
                self._wait(eng, t)
        for t in extra:
            self._wait(eng, t)

    def _commit(self, tk, reads, writes):
        for b in reads:
            b.r.append(tk)
            if len(b.r) > 16:
                last = {}
                for t in b.r:
                    if t.key not in last or last[t.key].val < t.val:
                        last[t.key] = t
                b.r = list(last.values())
        for b in writes:
            b.w = tk
            b.r = []

    def op(self, eng, fn, reads=(), writes=(), sig=True, extra=()):
        self._deps(eng, reads, writes, extra)
        if not sig:
            self.q[eng].append(("op", fn, None))
            self.pend[eng].append((tuple(reads), tuple(writes)))
            return None
        self.cnt[eng] += 1
        tk = Tk(eng, self.cnt[eng])
        self.q[eng].append(("op", fn, eng))
        for (r, w) in self.pend[eng]:
            self._commit(tk, r, w)
        self.pend[eng] = []
        self._commit(tk, reads, writes)
        return tk

    def dma(self, eng, out, in_, key, reads=(), writes=(), extra=()):
        self._deps(eng, reads, writes, extra)
        if key not in self.cnt:
            self.cnt[key] = 0
            self.dma_keys.append(key)
        self.cnt[key] += 16
        tk = Tk(key, self.cnt[key])
        self.q[eng].append(("dma", out, in_, key))
        self._commit(tk, reads, writes)
        return tk

    def barrier(self, exclude_prefix="cv_"):
        for e in self.ENGS:
            assert not self.pend[e]
        keys = list(self.ENGS) + [k for k in self.dma_keys if not k.startswith(exclude_prefix)]
        for e in self.ENGS:
            for k in keys:
                if self.cnt[k] > 0:
                    self._wait(e, Tk(k, self.cnt[k]))

    def emit(self, final_waits=()):
        nc = self.nc
        sems = {}
        for k in list(self.ENGS) + self.dma_keys:
            sems[k] = nc.alloc_semaphore(name="s_" + k)
        for tk in final_waits:
            self._wait("sp", tk)
        q = self.q

        def replay(e, items):
            for it in items:
                if it[0] == "wait":
                    e.wait_ge(sems[it[1]], it[2])
                elif it[0] == "op":
                    ins = it[1](e)
                    if it[2] is not None:
                        ins.then_inc(sems[it[2]], 1)
                else:
                    e.dma_start(out=it[1], in_=it[2]).then_inc(sems[it[3]], 16)

        with nc.Block() as block:
            @block.tensor
            def _(e):
                replay(e, q["pe"])

            @block.scalar
            def _(e):
                replay(e, q["act"])

            @block.vector
            def _(e):
                replay(e, q["dve"])

            @block.gpsimd
            def _(e):
                replay(e, q["pool"])

            @block.sync
            def _(e):
                replay(e, q["sp"])


def MM(out, lhsT, rhs, start, stop):
    return lambda e: e.matmul(out, lhsT=lhsT, rhs=rhs, start=start, stop=stop)


def TR(out, in_, ident):
    return lambda e: e.transpose(out, in_, ident)


def ACT(out, in_, func, bias=None, scale=None, accum_out=None):
    kw = {}
    if bias is not None:
        kw["bias"] = bias
    if scale is not None:
        kw["scale"] = scale
    if accum_out is not None:
        kw["accum_out"] = accum_out
    return lambda e: e.activation(out=out, in_=in_, func=func, **kw)


def TS(out, in0, s1, s2, op0, op1=None):
    if op1 is None:
        return lambda e: e.tensor_scalar(out=out, in0=in0, scalar1=s1, scalar2=None, op0=op0)
    return lambda e: e.tensor_scalar(out=out, in0=in0, scalar1=s1, scalar2=s2, op0=op0, op1=op1)


def TT(out, in0, in1, op):
    return lambda e: e.tensor_tensor(out=out, in0=in0, in1=in1, op=op)


def STT(out, in0, scalar, in1, op0, op1):
    return lambda e: e.scalar_tensor_tensor(out=out, in0=in0, scalar=scalar, in1=in1, op0=op0, op1=op1)


def CP(out, in_):
    return lambda e: e.tensor_copy(out=out, in_=in_)


def MS(ap, v):
    return lambda e: e.memset(ap, v)


def SCAN(out, d0, d1, init):
    return lambda e: e.tensor_tensor_scan(out=out, data0=d0, data1=d1, initial=init, op0=ALU.mult, op1=ALU.add)


def RECIP(out, in_):
    return lambda e: e.reciprocal(out=out, in_=in_)


class Ring:
    def __init__(self, P, name, nslots, off, slot_bytes):
        self.P = P
        self.n = nslots
        self.slots = [P.sb(f"{name}{i}", [128, slot_bytes // 2], BF16, off=off + i * slot_bytes) for i in range(nslots)]
        self.bufs = [Buf(f"{name}{i}") for i in range(nslots)]
        self.keys = [f"rg_{name}{i}" for i in range(nslots)]
        self.reqs = []
        self.issued = 0

    def plan(self, dram_ap, nelem, src_bufs=()):
        self.reqs.append((dram_ap, nelem, tuple(src_bufs)))
        return len(self.reqs) - 1

    def get(self, k, ahead):
        lim = min(len(self.reqs), k + ahead + 1)
        while self.issued < lim:
            i = self.issued
            ap, nelem, src = self.reqs[i]
            s = i % self.n
            self.P.dma("sp", self.slots[s][:, 0:nelem], ap, self.keys[s], reads=list(src), writes=[self.bufs[s]])
            self.issued += 1
        s = k % self.n
        return self.slots[s], self.bufs[s]


def build_program(debug=False):
    nc = bass.Bass("TRN2", target_bir_lowering=False)
    P = Prog(nc)

    def din(name, shape, dt=F32):
        return nc.dram_tensor(name, list(shape), dt, kind="ExternalInput").ap()

    x_d = din("x", [NTOK, D])
    ctx_d = din("ctx", [NCTX, D])
    cvec_d = din("cvec", [128, 8, 2])
    modw_d = din("mod_w", [D, 6 * D])
    modb_d = din("mod_b2", [2, 6 * D])
    n1g_d = din("n1g", [128, 8])
    n2g_d = din("n2g", [128, 8])
    fg_d = din("final_g", [1, D])
    win_d = din("w_in", [D, 4608])
    lcw_d = din("lcw", [128, 8, 4])
    lcb_d = din("lcb", [128, 8])
    ga_d = din("ga", [128, 2048])
    gx_d = din("gx", [128, 2048])
    gab_d = din("gab", [128, 2, 8])
    gxb_d = din("gxb", [128, 2, 8])
    lam_d = din("lam", [128, 2, 8])
    wf_d = din("w_f", [512, D])
    wr_d = din("w_r", [D, D])
    wo_d = din("w_o", [D, D])
    wu_d = din("w_up", [D, 2 * DFF])
    fcw_d = din("fcw", [128, 22, 18])
    fcb_d = din("fcb", [128, 44])
    wd_d = din("w_down", [DFF, D])
    idf_d = din("identf", [128, 128])
    idb_d = din("identb", [128, 128], BF16)
    cc_d = din("cc", [128, 384], BF16)
    cs_d = din("cs", [8, 2, 128, 16, 512], BF16)
    cs2_d = din("cs2", [128, 32, 2], BF16)
    out_d = nc.dram_tensor("out", [NTOK, D], F32, kind="ExternalOutput").ap()

    win_s = nc.dram_tensor("win_s", [18, 128, 2048], BF16, kind="Internal").ap()
    wf_s = nc.dram_tensor("wf_s", [4, 128, 1024], BF16, kind="Internal").ap()
    wr_s = nc.dram_tensor("wr_s", [4, 128, 2048], BF16, kind="Internal").ap()
    wo_s = nc.dram_tensor("wo_s", [4, 128, 2048], BF16, kind="Internal").ap()
    wu_s = nc.dram_tensor("wu_s", [22, 128, 2048], BF16, kind="Internal").ap()
    wd_s = nc.dram_tensor("wd_s", [11, 128, 2048], BF16, kind="Internal").ap()
    x1_s = nc.dram_tensor("x1_s", [NTOK, D], F32, kind="ExternalOutput" if debug else "Internal").ap()
    if debug:
        dbg_zx = nc.dram_tensor("dbg_zx", [128, 8, 4100], BF16, kind="ExternalOutput").ap()
        dbg_y = nc.dram_tensor("dbg_y", [128, 4, 4096], BF16, kind="ExternalOutput").ap()
        dbg_zx0 = nc.dram_tensor("dbg_zx0", [128, 8, 4100], BF16, kind="ExternalOutput").ap()
        dbg_misc = nc.dram_tensor("dbg_misc", [128, 256], F32, kind="ExternalOutput").ap()

    ps = [nc.alloc_psum_tensor(f"ps{i}", [128, 512], F32) for i in range(8)]
    psb = [Buf(f"ps{i}") for i in range(8)]

    identf = P.sb("identf", [128, 128], F32)
    identb = P.sb("identb", [128, 128], BF16)
    cc = P.sb("cc", [128, 384], BF16)
    modc = P.sb("modc", [128, 48, 2], F32)
    S1 = P.sb("S1", [128, 2, 8], F32)
    B1 = P.sb("B1", [128, 2, 8], F32)
    S2 = P.sb("S2", [128, 8], F32)
    B2 = P.sb("B2", [128, 8], F32)
    n1g = P.sb("n1g", [128, 8], F32)
    n2g = P.sb("n2g", [128, 8], F32)
    lcw = P.sb("lcw", [128, 8, 4], F32)
    lcb = P.sb("lcb", [128, 8], F32)
    gab = P.sb("gab", [128, 2, 8], F32)
    gxb = P.sb("gxb", [128, 2, 8], F32)
    cneg = P.sb("cneg", [128, 2, 8], F32)
    cnegh = P.sb("cnegh", [128, 2, 8], F32)
    H0 = P.sb("H0", [128, 2, 8], F32)
    fcw = P.sb("fcw", [128, 22, 18], F32)
    fcb = P.sb("fcb", [128, 44], F32)
    small = P.sb("small", [128, 64], F32)
    G1ROW = P.sb("G1ROW", [128, D], F32)
    G2ROW = P.sb("G2ROW", [128, D], F32)
    FGROW = P.sb("FGROW", [128, D], F32)
    GA = P.sb("GA", [128, 2048], BF16)
    GX = P.sb("GX", [128, 2048], BF16)
    SEL = P.sb("SEL", [2, 128], F32)
    EPS_AP = P.sb("EPS_AP", [128, 1], F32)
    RSTD1 = P.sb("RSTD1", [128, 32], F32, off=P.offs["SEL"])
    RSTD2 = P.sb("RSTD2", [128, 32], F32, off=P.offs["SEL"] + 128)
    ONE_AP = P.sb("ONE_AP", [128, 1], F32)
    DYN = (P.sb_off + 63) // 64 * 64
    ZX_OFF = DYN
    Y_OFF = ZX_OFF + 65664
    ZF_OFF = Y_OFF + 32768
    T_OFF = ZF_OFF + 32768
    assert SB_END - T_OFF >= 51200, (SB_END - T_OFF)
    FGT = P.sb("FGT", [2, D], F32, off=T_OFF + 24576 + 256)

    ZX = P.sb("ZX", [128, 8, 4100], BF16, off=ZX_OFF)
    Y = P.sb("Y", [128, 4, 4096], BF16, off=Y_OFF)
    ZF = P.sb("ZF", [128, 32, 512], BF16, off=ZF_OFF)

    cb = {k: Buf(k) for k in ["ident", "cc", "modc", "modp", "lrup", "lrup2", "H0", "fcp", "rows", "gates", "sel", "fgt", "eps"]}
    smallb = [Buf(f"small{i}") for i in range(32)]
    r1b = [Buf(f"r1_{i}") for i in range(32)]
    r2b = [Buf(f"r2_{i}") for i in range(32)]
    zxb = [[Buf(f"zx{m}_{b}") for b in range(8)] for m in range(8)]
    zfb = [Buf(f"zf{n}") for n in range(32)]
    yb = [[Buf(f"y{g}_{b}") for b in range(8)] for g in range(4)]

    wb = {}

    def conv_dma(name, dst, src, npan, cw, per=4):
        bufs = []
        for i in range(0, npan, per):
            j = min(npan, i + per)
            b = Buf(f"{name}{i}")
            for m in range(i, j):
                tk = P.dma("pool", dst[m].rearrange("p (kc c) -> p kc c", c=cw), src[m], f"cv_{name}{i}")
            b.w = tk
            bufs += [b] * (j - i)
        wb[name] = bufs

    win_v = win_d.rearrange("(kc p) (m c) -> m p kc c", p=128, c=256)
    P.dma("sp", identf[:], idf_d, "c_idf", writes=[cb["ident"]])
    P.dma("sp", identb[:], idb_d, "c_idb", writes=[cb["ident"]])
    P.dma("sp", cc[:], cc_d, "c_cc", writes=[cb["cc"]])
    P.dma("sp", n1g[:], n1g_d, "c_n1g", writes=[cb["modp"]])
    P.dma("sp", n2g[:], n2g_d, "c_n2g", writes=[cb["modp"]])
    P.dma("sp", lcw[:], lcw_d, "c_lcw", writes=[cb["lrup"]])
    P.dma("sp", lcb[:], lcb_d, "c_lcb", writes=[cb["lrup"]])
    P.dma("sp", gab[:], gab_d, "c_gab", writes=[cb["lrup"]])
    P.dma("sp", gxb[:], gxb_d, "c_gxb", writes=[cb["lrup"]])
    P.dma("sp", cneg[:], lam_d, "c_lam", writes=[cb["lrup2"]])
    P.dma("sp", fcw[:], fcw_d, "c_fcw", writes=[cb["fcp"]])
    P.dma("sp", fcb[:], fcb_d, "c_fcb", writes=[cb["fcp"]])

    MW = P.sb("MW", [128, 8, 6144], BF16, off=ZX_OFF)
    MROW = P.sb("MROW", [2, 6144], F32, off=ZF_OFF)
    MBROW = P.sb("MBROW", [2, 6144], F32, off=T_OFF)
    CV = P.sb("CV", [128, 8, 2], F32, off=T_OFF + 24576)
    SCB = P.sb("SCB", [128, 8, 2], BF16, off=T_OFF + 24576 + 64)
    mwb = [Buf(f"mw{i}") for i in range(6)]
    b_mrow, b_mbrow, b_cv, b_scb = Buf("mrow"), Buf("mbrow"), Buf("cv"), Buf("scb")
    modw_v = modw_d.rearrange("(kc p) n -> p kc n", p=128)
    P.dma("sp", CV[:], cvec_d, "c_cv", writes=[b_cv])
    P.dma("sp", MBROW[:], modb_d, "c_mb", writes=[b_mbrow])
    for i in range(6):
        P.dma("pool", MW[:, :, i * 1024:(i + 1) * 1024], modw_v[:, :, i * 1024:(i + 1) * 1024], f"c_mw{i}", writes=[mwb[i]])
    P.dma("pool", GA[:], ga_d, "c_ga", writes=[cb["gates"]])
    P.dma("pool", GX[:], gx_d, "c_gx", writes=[cb["gates"]])
    conv_dma("win", win_s, win_v, 18, 256)

    P.op("act", ACT(SCB[:], CV[:], AF.Silu), reads=[b_cv], writes=[b_scb])
    for cbk in range(12):
        bank = cbk % 2
        for kc in range(8):
            P.op("pe", MM(ps[bank][0:2, :], SCB[:, kc, :], MW[:, kc, cbk * 512:(cbk + 1) * 512], kc == 0, kc == 7),
                 reads=[b_scb, mwb[cbk // 2]], writes=[psb[bank]], sig=(kc == 7))
        P.op("dve", TT(MROW[:, cbk * 512:(cbk + 1) * 512], ps[bank][0:2, :], MBROW[:, cbk * 512:(cbk + 1) * 512], ALU.add),
             reads=[psb[bank], b_mbrow], writes=[b_mrow])
    pst = ps[2][:, 0:96].rearrange("p (j v) -> p j v", v=2)
    for j in range(48):
        P.op("pe", TR(pst[:, j, :], MROW[0:2, j * 128:(j + 1) * 128], identf[0:2, 0:2]),
             reads=[b_mrow, cb["ident"]], writes=[psb[2]], sig=(j == 47))
    P.op("dve", CP(modc[:], pst), reads=[psb[2]], writes=[cb["modc"]])
    for v in range(2):
        P.op("dve", STT(S1[:, v, :], modc[:, 8:16, v], 1.0, n1g[:], ALU.add, ALU.mult), reads=[cb["modc"], cb["modp"]], writes=[cb["modp"]])
        P.op("dve", CP(B1[:, v, :], modc[:, 0:8, v]), reads=[cb["modc"]], writes=[cb["modp"]])
    P.op("dve", STT(S2[:], modc[:, 32:40, 0], 1.0, n2g[:], ALU.add, ALU.mult), reads=[cb["modc"], cb["modp"]], writes=[cb["modp"]])
    P.op("dve", CP(B2[:], modc[:, 24:32, 0]), reads=[cb["modc"]], writes=[cb["modp"]])
    P.op("dve", MS(SEL[:], 0.0), writes=[cb["sel"]])
    P.op("dve", MS(SEL[0:1, :], 1.0), writes=[cb["sel"]])
    P.op("dve", MS(FGT[:], 0.0), writes=[cb["fgt"]])
    P.dma("sp", FGT[0:1, :], fg_d, "c_fg", writes=[cb["fgt"]])
    for (row, src, c0, sb_) in ((G1ROW, MROW, 2048, b_mrow), (G2ROW, MROW, 5120, b_mrow), (FGROW, FGT, 0, cb["fgt"])):
        for hc in range(2):
            bank = 3 + hc
            P.op("pe", MM(ps[bank][:], SEL[:], src[:, c0 + hc * 512:c0 + (hc + 1) * 512], True, True),
                 reads=[cb["sel"], sb_], writes=[psb[bank]])
            P.op("act", ACT(row[:, hc * 512:(hc + 1) * 512], ps[bank][:], AF.Copy), reads=[psb[bank]], writes=[cb["rows"]])
    P.op("act", ACT(cneg[:], cneg[:], AF.Exp, scale=-1.0), reads=[cb["lrup2"]], writes=[cb["lrup2"]])
    P.op("act", ACT(cneg[:], cneg[:], AF.Ln, bias=1.0), reads=[cb["lrup2"]], writes=[cb["lrup2"]])
    P.op("dve", TS(cnegh[:], cneg[:], -4.0, None, ALU.mult), reads=[cb["lrup2"]], writes=[cb["lrup2"]])
    P.op("dve", TS(cneg[:], cneg[:], -8.0, None, ALU.mult), reads=[cb["lrup2"]], writes=[cb["lrup2"]])
    P.op("dve", TS(gab[:], gab[:], 0.5, None, ALU.mult), reads=[cb["lrup"]], writes=[cb["lrup"]])
    P.op("dve", TS(gxb[:], gxb[:], 0.5, None, ALU.mult), reads=[cb["lrup"]], writes=[cb["lrup"]])
    P.barrier()

    nstate = {"i": 0}

    def norm_stats(xt, xbuf, XS, xsb, JUNK, jb, rstd=None, rbuf=None, save=None, sbuf=None):
        i = nstate["i"]
        nstate["i"] += 1
        xs, xb_ = XS[i % len(XS)], xsb[i % len(XS)]
        if rstd is not None:
            P.op("dve", TS(xs[:], xt, rstd, None, ALU.mult), reads=[xbuf, rbuf], writes=[xb_])
            return (xs, xb_)
        sc = small[:, (2 * i) % 64:(2 * i) % 64 + 2]
        smb = smallb[i % 32]
        dst, dbuf = (save, sbuf) if save is not None else (sc[:, 1:2], smb)
        P.op("act", ACT(JUNK[:], xt, AF.Square, accum_out=sc[:, 0:1]), reads=[xbuf], writes=[jb, smb])
        P.op("act", ACT(dst, sc[:, 0:1], AF.Sqrt, bias=EPS_AP[:, 0:1], scale=1.0 / D), reads=[smb, cb["eps"]], writes=[dbuf])
        P.op("dve", RECIP(dst, dst), reads=[dbuf], writes=[dbuf])
        P.op("dve", TS(xs[:], xt, dst, None, ALU.mult), reads=[xbuf, dbuf], writes=[xb_])
        return (xs, xb_)

    def norm_finish(hdl, Scol, Bcol, out_fn, obuf, tr_banks):
        xs, xb_ = hdl
        for half in range(2):
            bank = tr_banks[half]
            for q in range(4):
                kc = half * 4 + q
                P.op("pe", TR(ps[bank][:, q * 128:(q + 1) * 128], xs[:, kc * 128:(kc + 1) * 128], identf[:]),
                     reads=[xb_, cb["ident"]], writes=[psb[bank]], sig=(q == 3))
            for q in range(4):
                kc = half * 4 + q
                if q % 2 == 0:
                    P.op("act", ACT(out_fn(kc), ps[bank][:, q * 128:(q + 1) * 128], AF.Identity, bias=Bcol[:, kc:kc + 1], scale=Scol[:, kc:kc + 1]),
                         reads=[psb[bank], cb["modp"]], writes=[obuf])
                else:
                    P.op("dve", TS(out_fn(kc), ps[bank][:, q * 128:(q + 1) * 128], Scol[:, kc:kc + 1], Bcol[:, kc:kc + 1], ALU.mult, ALU.add),
                         reads=[psb[bank], cb["modp"]], writes=[obuf])

    def norm_to_fm(xt, xbuf, Scol, Bcol, out_fn, obuf, XS, xsb, JUNK, jb, tr_banks):
        norm_finish(norm_stats(xt, xbuf, XS, xsb, JUNK, jb), Scol, Bcol, out_fn, obuf, tr_banks)

    class Prep:
        def __init__(self, n, stats_fn, finish_fn, lead=1):
            self.n, self.stats_fn, self.finish_fn, self.lead = n, stats_fn, finish_fn, lead
            self.j = 0
            self.q = []

        def step(self):
            if self.q and (len(self.q) >= self.lead or self.j >= self.n):
                jj, h = self.q.pop(0)
                if h is not None:
                    self.finish_fn(jj, h)
            if self.j < self.n:
                self.q.append((self.j, self.stats_fn(self.j)))
                self.j += 1

        def done(self):
            return self.j >= self.n and not self.q

        def flush(self):
            while not self.done():
                self.step()

    P.op("dve", MS(EPS_AP[:], EPS), writes=[cb["eps"]])

    P.op("dve", MS(ONE_AP[:], 0.25), writes=[cb["eps"]])

    XA = [P.sb(f"XA{i}", [128, 4, D], F32, off=Y_OFF + i * 16384) for i in range(2)]
    xab = [[Buf(f"xa{i}_{j}") for j in range(4)] for i in range(2)]
    WA = P.sb("WA", [128, 6, 8, 256], BF16, off=T_OFF)
    b_wa = Buf("wa")
    HL = [P.sb(f"HL{i}", [128, 8, 512], BF16, off=T_OFF + 24576 + i * 8192) for i in range(2)]
    hlb = [[Buf(f"hl{i}_{j}") for j in range(4)] for i in range(2)]
    JUNK = P.sb("JUNK", [128, D], BF16, off=T_OFF + 40960)
    jb = Buf("junk")
    XS = [P.sb(f"XS{i}", [128, D], F32, off=T_OFF + 43008 + i * 4096) for i in range(3)]
    xsb = [Buf(f"xs{i}") for i in range(3)]
    ZXC = P.sb("ZXC", [128, 8, 260], BF16, off=ZF_OFF)
    b_zxc = Buf("zxc")
    co = ZF_OFF + 4160
    cU = P.sb("cU", [128, 256], F32, off=co)
    cUB = P.sb("cUB", [128, 256], BF16, off=co + 1024)
    cHF = P.sb("cHF", [128, 256], F32, off=co + 1536)
    cR = P.sb("cR", [128, 256], F32, off=co + 2560)
    cI = P.sb("cI", [128, 256], F32, off=co + 3584)
    cA = P.sb("cA", [128, 256], F32, off=co + 4608)
    cHB = P.sb("cHB", [128, 256], F32, off=co + 5632)
    cCAR = P.sb("cCAR", [128, 16], F32, off=co + 6656)

    P.dma("sp", WA[:].rearrange("p m kc c -> p m (kc c)"), win_s[0:6].rearrange("m p f -> p m f"), "c_wa", reads=[wb["win"][0], wb["win"][4]], writes=[b_wa])
    P.op("dve", MS(ZXC[:], 0.0), writes=[b_zxc])

    P.dma("sp", XA[0][:, 0:2, :], ctx_d.rearrange("(j p) d -> p j d", p=128), "c_ctx", writes=[xab[0][0], xab[0][1]])
    for j in range(2):
        norm_to_fm(XA[0][:, j, :], xab[0][j], S1[:, 1, :], B1[:, 1, :],
                   lambda kc, j=j: HL[0][:, kc, j * 128:(j + 1) * 128], hlb[0][j], XS, xsb, JUNK, jb, (0, 1))
    for m in range(8):
        bank = 2 + m % 2
        for kc in range(8):
            P.op("pe", MM(ps[bank][:, 0:256], WA[:, 2 + m // 2, kc, (m % 2) * 128:(m % 2) * 128 + 128], HL[0][:, kc, 0:256], kc == 0, kc == 7),
                 reads=[b_wa, hlb[0][0], hlb[0][1]], writes=[psb[bank]], sig=(kc == 7))
        P.op("act", ACT(ZXC[:, m, 2:258], ps[bank][:, 0:256], AF.Copy), reads=[psb[bank]], writes=[b_zxc])
    zo = ZX_OFF
    kU = P.sb("kU", [128, 8, 256], F32, off=zo)
    kUB = P.sb("kUB", [128, 8, 256], BF16, off=zo + 8192)
    kR = P.sb("kR", [128, 8, 256], F32, off=zo + 12288)
    kI = P.sb("kI", [128, 8, 256], F32, off=zo + 20480)
    kA = P.sb("kA", [128, 8, 256], F32, off=zo + 28672)
    kH = P.sb("kH", [128, 8, 256], F32, off=zo + 36864)
    kub = [Buf(f"ku{h}") for h in range(8)]
    b_kub, b_kr, b_ki, b_ka, b_kh = Buf("kub"), Buf("kr"), Buf("ki"), Buf("ka"), Buf("kh")
    for k in range(4):
        for h in range(8):
            if k == 0:
                P.op("dve", TS(kU[:, h, :], ZXC[:, h, 0:256], lcw[:, h, 0:1], lcb[:, h:h + 1], ALU.mult, ALU.add),
                     reads=[b_zxc, cb["lrup"]], writes=[kub[h]])
            else:
                P.op("dve", STT(kU[:, h, :], ZXC[:, h, k:k + 256], lcw[:, h, k:k + 1], kU[:, h, :], ALU.mult, ALU.add),
                     reads=[b_zxc, cb["lrup"], kub[h]], writes=[kub[h]])
    P.op("pool", CP(kUB[:], kU[:]), reads=kub, writes=[b_kub])
    for d in range(2):
        for h in range(8):
            gsl = slice((d * 8 + h) * 128, (d * 8 + h + 1) * 128)
            bR, bI = h // 2, 4 + h // 2
            csl = slice((h % 2) * 256, (h % 2) * 256 + 256)
            P.op("pe", MM(ps[bR][:, csl], GA[:, gsl], kUB[:, h, :], True, True), reads=[cb["gates"], b_kub], writes=[psb[bR]])
            P.op("pe", MM(ps[bI][:, csl], GX[:, gsl], kUB[:, h, :], True, True), reads=[cb["gates"], b_kub], writes=[psb[bI]])
        for h in range(8):
            bR, bI = h // 2, 4 + h // 2
            csl = slice((h % 2) * 256, (h % 2) * 256 + 256)
            P.op("act", ACT(kR[:, h, :], ps[bR][:, csl], AF.Tanh, bias=gab[:, d, h:h + 1], scale=0.5), reads=[psb[bR], cb["lrup"]], writes=[b_kr])
            P.op("act", ACT(kI[:, h, :], ps[bI][:, csl], AF.Tanh, bias=gxb[:, d, h:h + 1], scale=0.5), reads=[psb[bI], cb["lrup"]], writes=[b_ki])
        for h in range(8):
            P.op("act", ACT(kA[:, h, :], kR[:, h, :], AF.Exp, bias=cnegh[:, d, h:h + 1], scale=cnegh[:, d, h:h + 1]), reads=[b_kr, cb["lrup2"]], writes=[b_ka])
        for h in range(8):
            P.op("act", ACT(kR[:, h, :], kR[:, h, :], AF.Exp, bias=cneg[:, d, h:h + 1], scale=cneg[:, d, h:h + 1]), reads=[b_kr, cb["lrup2"]], writes=[b_kr])
        P.op("act", ACT(kR[:], kR[:], AF.Sqrt, bias=ONE_AP[:, 0:1], scale=-0.25), reads=[b_kr, cb["eps"]], writes=[b_kr])
        P.op("dve", STT(kI[:], kI[:], 1.0, kR[:], ALU.add, ALU.mult), reads=[b_ki, b_kr], writes=[b_ki])
        P.op("dve", TT(kI[:], kI[:], kU[:], ALU.mult), reads=[b_ki] + kub, writes=[b_ki])
        for h in range(8):
            if d == 0:
                P.op("dve", SCAN(kH[:, h, :], kA[:, h, :], kI[:, h, :], 0.0), reads=[b_ka, b_ki], writes=[b_kh])
            else:
                P.op("dve", SCAN(kH[:, h, :][:, ::-1], kA[:, h, :][:, ::-1], kI[:, h, :][:, ::-1], 0.0), reads=[b_ka, b_ki], writes=[b_kh])
        P.op("dve", CP(H0[:, d, :], kH[:, :, 255] if d == 0 else kH[:, :, 0]), reads=[b_kh], writes=[cb["H0"]])
    P.barrier()
    conv_dma("wf", wf_s, wf_d.rearrange("(kc p) (m c) -> m p kc c", p=128, c=256), 4, 256)
    conv_dma("wr", wr_s, wr_d.rearrange("(kc p) (m c) -> m p kc c", p=128, c=256), 4, 256)
    conv_dma("wo", wo_s, wo_d.rearrange("(q kc p) n -> q p kc n", p=128, kc=2), 4, 1024)
    conv_dma("wu", wu_s, wu_d.rearrange("(kc p) (m c) -> m p kc c", p=128, c=256), 22, 256)
    conv_dma("wd", wd_s, wd_d.rearrange("(q kc p) n -> q p kc n", p=128, kc=2), 11, 1024)
    b_zxpad = Buf("zxpad")
    P.op("pool", MS(ZX[:, :, 0:2], 0.0), writes=[b_zxpad])
    P.op("pool", MS(ZX[:, :, 4098:4100], 0.0), writes=[b_zxpad])

    x_v = x_d.rearrange("(b j p) d -> b p j d", p=128, j=4)

    def mk_prepA(blk):
        s_ = blk % 2

        def st(j):
            if j == 0:
                P.dma("sp", XA[s_][:], x_v[blk], f"c_xa{s_}", writes=xab[s_])
            return norm_stats(XA[s_][:, j, :], xab[s_][j], XS, xsb, JUNK, jb, save=RSTD1[:, blk * 4 + j:blk * 4 + j + 1], sbuf=r1b[blk * 4 + j])

        def fin(j, hdl):
            norm_finish(hdl, S1[:, 0, :], B1[:, 0, :], lambda kc, j=j: HL[s_][:, kc, j * 128:(j + 1) * 128], hlb[s_][j], (0, 1))
        return Prep(4, st, fin, lead=2)

    pa = mk_prepA(0)
    pa.flush()
    for blk in range(8):
        s = blk % 2
        pa = mk_prepA(blk + 1) if blk + 1 < 8 else None
        for m in range(8):
            bank = 2 + m % 3
            for kc in range(8):
                P.op("pe", MM(ps[bank][:], WA[:, 2 + m // 2, kc, (m % 2) * 128:(m % 2) * 128 + 128], HL[s][:, kc, :], kc == 0, kc == 7),
                     reads=[b_wa] + hlb[s], writes=[psb[bank]], sig=(kc == 7))
            if m % 2 == 0:
                P.op("act", ACT(ZX[:, m, 2 + blk * 512:2 + (blk + 1) * 512], ps[bank][:], AF.Copy), reads=[psb[bank]], writes=[zxb[m][blk]])
            else:
                P.op("dve", CP(ZX[:, m, 2 + blk * 512:2 + (blk + 1) * 512], ps[bank][:]), reads=[psb[bank]], writes=[zxb[m][blk]])
            if pa is not None:
                pa.step()
        if pa is not None:
            pa.flush()
        for j in range(4):
            bank = 5 + j % 3
            for half in range(2):
                for kc in range(8):
                    P.op("pe", MM(ps[bank][:, half * 256:(half + 1) * 256], HL[s][:, kc, j * 128:(j + 1) * 128], WA[:, half, kc, :], kc == 0, kc == 7),
                         reads=[b_wa, hlb[s][j]], writes=[psb[bank]], sig=(kc == 7 and half == 1))
            n = blk * 4 + j
            if j % 2 == 0:
                P.op("dve", CP(ZF[:, n, :], ps[bank][:]), reads=[psb[bank]], writes=[zfb[n]])
            else:
                P.op("act", ACT(ZF[:, n, :], ps[bank][:], AF.Copy), reads=[psb[bank]], writes=[zfb[n]])
    P.barrier()
    if debug:
        P.dma("sp", dbg_zx0, ZX[:], "c_dbg0", reads=[zxb[m][b] for m in range(8) for b in range(8)])
        P.barrier()

    CSB = [P.sb(f"CSB{i}", [128, 16, 512], BF16, off=T_OFF + i * 16384) for i in range(3)]
    csb = [Buf(f"csb{i}") for i in range(3)]
    PQB = [P.sb(f"PQB{i}", [128, 512], BF16, off=T_OFF + 49152 + i * 1024) for i in range(2)]
    pqb = [Buf(f"pqb{i}") for i in range(2)]
    CS2 = P.sb("CS2", [128, 32, 2], BF16, off=T_OFF + 51200)
    b_cs2 = Buf("cs2")
    P.dma("sp", CS2[:], cs2_d, "c_cs2", writes=[b_cs2])
    YSCALE = float(1.0 / np.sqrt(4096.0 * 128.0))
    nreq = 16
    issued = 0

    def cs_fetch(upto):
        nonlocal issued
        while issued < min(nreq, upto + 1):
            i = issued
            P.dma("sp", CSB[i % 3][:], cs_d[i // 2, i % 2], f"c_cs{i % 3}", writes=[csb[i % 3]])
            issued += 1

    def ybufs(g, lo, hi):
        return [yb[g][b_] for b_ in range(lo // 512, hi // 512 + 1)]

    ev = 0
    for kb in range(8):
        k0 = kb * 256
        for hh in range(2):
            r = kb * 2 + hh
            cs_fetch(r + 2)
            sl = r % 3
            for g in range(4):
                for nci in range(16):
                    n = hh * 16 + nci
                    P.op("pe", MM(ps[g][:], ZF[:, n, g * 128:(g + 1) * 128], CSB[sl][:, nci, :], n == 0, n == 31),
                         reads=[zfb[n], csb[sl]], writes=[psb[g]], sig=(nci == 15))
        for g in range(4):
            pq, pb = PQB[ev % 2], pqb[ev % 2]
            ybank = 4 + 2 * (ev % 2)
            ev += 1
            P.op("act", ACT(pq[:], ps[g][:], AF.Copy), reads=[psb[g]], writes=[pb])
            P.op("pe", MM(ps[ybank][:, 0:256], cc[:, 0:128], pq[:, 0:256], True, False), reads=[cb["cc"], pb], writes=[psb[ybank]], sig=False)
            P.op("pe", MM(ps[ybank][:, 0:256], cc[:, 128:256], pq[:, 256:512], False, True), reads=[cb["cc"], pb], writes=[psb[ybank]])
            P.op("pe", MM(ps[ybank + 1][:, 0:256], cc[:, 0:128], pq[:, 0:256], True, False), reads=[cb["cc"], pb], writes=[psb[ybank + 1]], sig=False)
            P.op("pe", MM(ps[ybank + 1][:, 0:256], cc[:, 256:384], pq[:, 256:512], False, True), reads=[cb["cc"], pb], writes=[psb[ybank + 1]])
            P.op("dve", TS(Y[:, g, k0:k0 + 256], ps[ybank][:, 0:256], YSCALE, None, ALU.mult), reads=[psb[ybank]], writes=ybufs(g, k0, k0 + 255))
            s0 = 1 if kb == 0 else 0
            lo_, hi_ = 4096 - k0 - 255, 4096 - k0 - s0
            P.op("act", ACT(Y[:, g, lo_:hi_ + 1][:, ::-1], ps[ybank + 1][:, s0:256], AF.Copy, scale=YSCALE), reads=[psb[ybank + 1]], writes=ybufs(g, lo_, hi_))
    for g in range(4):
        for n in range(32):
            P.op("pe", MM(ps[g][:, 0:2], ZF[:, n, g * 128:(g + 1) * 128], CS2[:, n, :], n == 0, n == 31),
                 reads=[zfb[n], b_cs2], writes=[psb[g]], sig=(n == 31))
    for g in range(4):
        pq, pb = PQB[ev % 2], pqb[ev % 2]
        ybank = 4 + 2 * (ev % 2)
        ev += 1
        P.op("act", ACT(pq[:, 0:2], ps[g][:, 0:2], AF.Copy), reads=[psb[g]], writes=[pb])
        P.op("pe", MM(ps[ybank][:, 0:2], cc[:, 0:128], pq[:, 0:2], True, True), reads=[cb["cc"], pb], writes=[psb[ybank]])
        P.op("dve", TS(Y[:, g, 2048:2049], ps[ybank][:, 0:1], YSCALE, None, ALU.mult), reads=[psb[ybank]], writes=[yb[g][4]])
    P.barrier()

    lo = ZF_OFF
    LU = P.sb("LU", [128, 4096], F32, off=lo)
    LUB = P.sb("LUB", [128, 4096], BF16, off=lo + 16384)
    LH1 = P.sb("LH1", [128, 4096], F32, off=lo + 24576)
    NSET = 3
    lsets = []
    for i in range(NSET):
        o_ = lo + 40960 + i * 12288
        lsets.append(dict(R=P.sb(f"LR{i}", [128, 1024], F32, off=o_), I=P.sb(f"LI{i}", [128, 1024], F32, off=o_ + 4096),
                          A=P.sb(f"LA{i}", [128, 1024], F32, off=o_ + 8192),
                          bR=Buf(f"lR{i}"), bI=Buf(f"lI{i}"), bA=Buf(f"lA{i}")))
    o_ = lo + 40960 + NSET * 12288
    LHB = P.sb("LHB", [128, 1024], F32, off=o_)
    LCAR = P.sb("LCAR", [128, 16], F32, off=o_ + 4096)
    LDG = P.sb("LDG", [128, 2, 4, 128], BF16, off=o_ + 4096 + 64)
    assert o_ + 4096 + 64 + 2048 <= SB_END
    ldgb = [Buf("ldg0"), Buf("ldg1")]
    b_lhb, b_lcar = Buf("lhb"), Buf("lcar")
    ub_ = [Buf(f"lu{i}") for i in range(4)]
    ubb_ = [Buf(f"lub{i}") for i in range(4)]
    h1b = [Buf(f"lh1{i}") for i in range(4)]
    TH, TW, NH = 1024, 512, 4

    def conv_seg(h, hf):
        ds = h % 2
        for ti in (2 * hf, 2 * hf + 1):
            bank = 4 + ti % 4
            for k in range(4):
                P.op("pe", MM(ps[bank][:], LDG[:, ds, k, :], ZX[:, h, ti * TW + k:ti * TW + k + TW], k == 0, k == 3),
                     reads=[ldgb[ds], b_zxpad] + zxb[h], writes=[psb[bank]], sig=(k == 3))
            P.op("act", ACT(LU[:, ti * TW:(ti + 1) * TW], ps[bank][:], AF.Identity, bias=lcb[:, h:h + 1]),
                 reads=[psb[bank], cb["lrup"]], writes=[ub_[hf]])
        P.op("pool", CP(LUB[:, hf * TH:(hf + 1) * TH], LU[:, hf * TH:(hf + 1) * TH]), reads=[ub_[hf]], writes=[ubb_[hf]])

    def diag_build(h):
        P.op("pool", TT(LDG[:, h % 2], identb[:].unsqueeze(1).broadcast_to([128, 4, 128]),
                        lcw[:, h, :].unsqueeze(2).broadcast_to([128, 4, 128]), ALU.mult),
             reads=[cb["ident"], cb["lrup"]], writes=[ldgb[h % 2]])

    diag_build(0)
    for hf in range(4):
        conv_seg(0, hf)
    gseg = 0
    for h in range(8):
        if h + 1 < 8:
            diag_build(h + 1)
        dirs = (0, 1) if h % 2 == 0 else (1, 0)
        segs = []
        for di, d in enumerate(dirs):
            order = list(range(NH)) if d == 0 else list(range(NH - 1, -1, -1))
            for oi, hf in enumerate(order):
                segs.append((di, d, oi, hf))
        for pair in range(0, 8, 2):
            info = []
            for s_ in (pair, pair + 1):
                di, d, oi, hf = segs[s_]
                S_ = lsets[gseg % NSET]
                gseg += 1
                info.append((s_, di, d, oi, hf, S_))
                R, I_, A = S_["R"], S_["I"], S_["A"]
                t0 = hf * TH
                gsl = slice((d * 8 + h) * 128, (d * 8 + h + 1) * 128)
                bR, bI = 2 * (s_ % 2), 2 * (s_ % 2) + 1
                for ti in range(TH // TW):
                    c0 = t0 + ti * TW
                    P.op("pe", MM(ps[bR][:], GA[:, gsl], LUB[:, c0:c0 + TW], True, True), reads=[cb["gates"], ubb_[hf]], writes=[psb[bR]])
                    P.op("pe", MM(ps[bI][:], GX[:, gsl], LUB[:, c0:c0 + TW], True, True), reads=[cb["gates"], ubb_[hf]], writes=[psb[bI]])
                    P.op("act", ACT(R[:, ti * TW:(ti + 1) * TW], ps[bR][:], AF.Tanh, bias=gab[:, d, h:h + 1], scale=0.5),
                         reads=[psb[bR], cb["lrup"]], writes=[S_["bR"]])
                    P.op("act", ACT(I_[:, ti * TW:(ti + 1) * TW], ps[bI][:], AF.Tanh, bias=gxb[:, d, h:h + 1], scale=0.5),
                         reads=[psb[bI], cb["lrup"]], writes=[S_["bI"]])
                P.op("act", ACT(A[:], R[:], AF.Exp, bias=cnegh[:, d, h:h + 1], scale=cnegh[:, d, h:h + 1]), reads=[S_["bR"], cb["lrup2"]], writes=[S_["bA"]])
                P.op("dve", TT(R[:], A[:], A[:], ALU.mult), reads=[S_["bA"], S_["bR"]], writes=[S_["bR"]])
            for (s_, di, d, oi, hf, S_) in info:
                P.op("act", ACT(S_["R"][:], S_["R"][:], AF.Sqrt, bias=ONE_AP[:, 0:1], scale=-0.25), reads=[S_["bR"], cb["eps"]], writes=[S_["bR"]])
            for (s_, di, d, oi, hf, S_) in info:
                R, I_, A = S_["R"], S_["I"], S_["A"]
                t0 = hf * TH
                P.op("dve", STT(I_[:], I_[:], 1.0, R[:], ALU.add, ALU.mult), reads=[S_["bI"], S_["bR"]], writes=[S_["bI"]])
                P.op("dve", TT(I_[:], I_[:], LU[:, t0:t0 + TH], ALU.mult), reads=[S_["bI"], ub_[hf]], writes=[S_["bI"]])
                rev = (lambda ap: ap[:, ::-1]) if d == 1 else (lambda ap: ap)
                if di == 0:
                    if oi == 0:
                        init = H0[:, d, h:h + 1]
                    else:
                        init = LH1[:, t0 - 1:t0] if d == 0 else LH1[:, t0 + TH:t0 + TH + 1]
                    prev = [] if oi == 0 else [h1b[hf - 1 if d == 0 else hf + 1]]
                    P.op("dve", SCAN(rev(LH1[:, t0:t0 + TH]), rev(A[:]), rev(I_[:]), init),
                         reads=[S_["bA"], S_["bI"], cb["H0"]] + prev, writes=[h1b[hf]])
                else:
                    init = H0[:, d, h:h + 1] if oi == 0 else LCAR[:, 0:1]
                    P.op("dve", SCAN(rev(LHB[:]), rev(A[:]), rev(I_[:]), init),
                         reads=[S_["bA"], S_["bI"], cb["H0"], b_lcar], writes=[b_lhb])
                    if oi < NH - 1:
                        edge = LHB[:, TH - 1:TH] if d == 0 else LHB[:, 0:1]
                        P.op("dve", CP(LCAR[:, 0:1], edge), reads=[b_lhb], writes=[b_lcar])
                    P.op("dve", TT(ZX[:, h, 2 + t0:2 + t0 + TH], LH1[:, t0:t0 + TH], LHB[:], ALU.add),
                         reads=[h1b[hf], b_lhb], writes=[zxb[h][2 * hf], zxb[h][2 * hf + 1]])
                    if h + 1 < 8:
                        conv_seg(h + 1, hf)
    P.barrier()
    if debug:
        P.dma("sp", dbg_zx, ZX[:], "c_dbg1", reads=[zxb[m][b] for m in range(8) for b in range(8)])
        P.dma("sp", dbg_y, Y[:], "c_dbg2", reads=[yb[g][b] for g in range(4) for b in range(8)])
        P.dma("sp", dbg_misc[:, 0:96], modc[:].rearrange("p j v -> p (j v)"), "c_dbg3", reads=[cb["modc"]])
        P.dma("sp", dbg_misc[:, 96:112], H0[:].rearrange("p d h -> p (d h)"), "c_dbg4", reads=[cb["H0"]])
        P.barrier()

    bo = ZF_OFF
    XQ = [P.sb(f"XQ{i}", [128, D], F32, off=bo + i * 4096) for i in range(6)]
    xqb = [Buf(f"xq{i}") for i in range(6)]
    xq_state = {"i": 0}

    def xq_load(q):
        i = xq_state["i"] % 6
        xq_state["i"] += 1
        P.dma("sp", XQ[i][:], x_q[q], f"c_xq{i}", writes=[xqb[i]])
        return XQ[i], xqb[i]
    HLB = [P.sb(f"HLB{i}", [128, 8, 512], BF16, off=bo + 24576 + i * 8192) for i in range(2)]
    hlbb = [[Buf(f"hlb{i}_{j}") for j in range(4)] for i in range(2)]
    MG = P.sb("MG", [128, 8, 512], BF16, off=bo + 40960)
    mgb = [Buf(f"mg{m}") for m in range(8)]
    SG = [P.sb(f"SG{i}", [128, 512], F32, off=P.offs["GA"] + i * 2048) for i in range(4)]
    sgb = [Buf(f"sg{i}") for i in range(4)]
    GT = [P.sb(f"GT{i}", [128, 512], F32, off=bo + 49152 + i * 2048) for i in range(2)]
    gtb = [Buf(f"gt{i}") for i in range(2)]
    XSB = [P.sb(f"XSB{i}", [128, D], F32, off=bo + 53248 + i * 4096) for i in range(2)]
    xsbb = [Buf(f"xsb{i}") for i in range(2)]
    JUNKB = P.sb("JUNKB", [128, D], BF16, off=bo + 61440)
    jbb = Buf("junkb")
    ring_off = bo + 63488
    NSB = min(8, (SB_END - ring_off) // 4096)
    assert NSB >= 6, NSB
    ringB = Ring(P, "rb", NSB, ring_off, 4096)
    planB = []
    for blk in range(8):
        d_ = {}
        d_["y"] = [ringB.plan(win_s[6 + i], 2048, [wb["win"][6 + i]]) for i in range(4)]
        d_["mp"] = []
        for mp in range(4):
            d_["mp"].append((
                ringB.plan(win_s[10 + mp], 2048, [wb["win"][10 + mp]]),
                ringB.plan(win_s[14 + mp], 2048, [wb["win"][14 + mp]]),
                ringB.plan(wf_s[mp], 1024, [wb["wf"][mp]]),
                ringB.plan(wr_s[mp], 2048, [wb["wr"][mp]])))
        d_["o"] = [ringB.plan(wo_s[i], 2048, [wb["wo"][i]]) for i in range(4)]
        planB.append(d_)
    AH = NSB - 4
    x1b = [Buf(f"x1s{q}") for q in range(32)]
    x_q = x_d.rearrange("(q p) d -> q p d", p=128)
    x1_q = x1_s.rearrange("(q p) d -> q p d", p=128)

    def mk_prepB(blk):
        def st(j):
            q = blk * 4 + j
            xt_, xb__ = xq_load(q)
            return norm_stats(xt_[:], xb__, XSB, xsbb, JUNKB, jbb, rstd=RSTD1[:, q:q + 1], rbuf=r1b[q])

        def fin(j, hdl):
            norm_finish(hdl, S1[:, 0, :], B1[:, 0, :], lambda kc, j=j: HLB[blk % 2][:, kc, j * 128:(j + 1) * 128], hlbb[blk % 2][j], (0, 1))
        return Prep(4, st, fin)

    pb_ = mk_prepB(0)
    pb_.flush()
    for blk in range(8):
        t0 = blk * 512
        hl, hlbf = HLB[blk % 2], hlbb[blk % 2]
        pl = planB[blk]
        pb_ = mk_prepB(blk + 1) if blk + 1 < 8 else None
        for m in range(8):
            slot, sbuf_ = ringB.get(pl["y"][m // 2], AH)
            wv = slot[:, 0:2048].rearrange("p (kc c) -> p kc c", c=256)
            bank = 2 + m % 2
            for kc in range(8):
                P.op("pe", MM(ps[bank][:], wv[:, kc, (m % 2) * 128:(m % 2) * 128 + 128], hl[:, kc, :], kc == 0, kc == 7),
                     reads=[sbuf_] + hlbf, writes=[psb[bank]], sig=(kc == 7))
            gt, gb = GT[m % 2], gtb[m % 2]
            P.op("act", ACT(gt[:], ps[bank][:], AF.Gelu_apprx_tanh), reads=[psb[bank]], writes=[gb])
            zsl = ZX[:, m, 2 + t0:2 + t0 + 512]
            P.op("dve", TT(zsl, gt[:], zsl, ALU.mult), reads=[gb, zxb[m][blk]], writes=[zxb[m][blk]])
        for m in range(8):
            mp, sub = m // 2, (m % 2) * 128
            kgf, kgr, kwf, kwr = pl["mp"][mp]
            s_gf, b_gf = ringB.get(kgf, AH)
            s_gr, b_gr = ringB.get(kgr, AH)
            s_wf, b_wf = ringB.get(kwf, AH)
            s_wr, b_wr = ringB.get(kwr, AH)
            v_gf = s_gf[:, 0:2048].rearrange("p (kc c) -> p kc c", c=256)
            v_gr = s_gr[:, 0:2048].rearrange("p (kc c) -> p kc c", c=256)
            v_wf = s_wf[:, 0:1024].rearrange("p (kc c) -> p kc c", c=256)
            v_wr = s_wr[:, 0:2048].rearrange("p (kc c) -> p kc c", c=256)
            for kc in range(8):
                P.op("pe", MM(ps[4][:], v_gf[:, kc, sub:sub + 128], hl[:, kc, :], kc == 0, kc == 7), reads=[b_gf] + hlbf, writes=[psb[4]], sig=(kc == 7))
            for kc in range(8):
                P.op("pe", MM(ps[5][:], v_gr[:, kc, sub:sub + 128], hl[:, kc, :], kc == 0, kc == 7), reads=[b_gr] + hlbf, writes=[psb[5]], sig=(kc == 7))
            for kc in range(4):
                P.op("pe", MM(ps[6][:], v_wf[:, kc, sub:sub + 128], Y[:, kc, t0:t0 + 512], kc == 0, kc == 3), reads=[b_wf, yb[kc][blk]], writes=[psb[6]], sig=(kc == 3))
            for kc in range(8):
                P.op("pe", MM(ps[7][:], v_wr[:, kc, sub:sub + 128], ZX[:, kc, 2 + t0:2 + t0 + 512], kc == 0, kc == 7), reads=[b_wr, zxb[kc][blk]], writes=[psb[7]], sig=(kc == 7))
            sa, sab = SG[(2 * m) % 4], sgb[(2 * m) % 4]
            sr, srb = SG[(2 * m + 1) % 4], sgb[(2 * m + 1) % 4]
            P.op("act", ACT(sa[:], ps[4][:], AF.Sigmoid), reads=[psb[4]], writes=[sab])
            P.op("act", ACT(sr[:], ps[5][:], AF.Sigmoid), reads=[psb[5]], writes=[srb])
            P.op("dve", TT(sa[:], sa[:], ps[6][:], ALU.mult), reads=[sab, psb[6]], writes=[sab])
            P.op("dve", TT(sr[:], sr[:], ps[7][:], ALU.mult), reads=[srb, psb[7]], writes=[srb])
            P.op("dve", TT(MG[:, m, :], sa[:], sr[:], ALU.add), reads=[sab, srb], writes=[mgb[m]])
            if pb_ is not None and m >= 2:
                pb_.step()
        if pb_ is not None:
            pb_.flush()
        so = [ringB.get(pl["o"][i], AH) for i in range(4)]
        ei = 0
        xrs = [xq_load(blk * 4 + j) for j in range(4)]
        for j in range(4):
            q = blk * 4 + j
            xr, xrbf = xrs[j]
            for hc in range(2):
                bank = 2 + ei % 2
                for kc in range(8):
                    wv = so[kc // 2][0][:, 0:2048].rearrange("p (kc c) -> p kc c", c=1024)
                    P.op("pe", MM(ps[bank][:], MG[:, kc, j * 128:(j + 1) * 128], wv[:, kc % 2, hc * 512:(hc + 1) * 512], kc == 0, kc == 7),
                         reads=[mgb[kc], so[kc // 2][1]], writes=[psb[bank]], sig=(kc == 7))
                tt, ttb = GT[ei % 2], gtb[ei % 2]
                ei += 1
                P.op("dve", TT(tt[:], ps[bank][:], G1ROW[:, hc * 512:(hc + 1) * 512], ALU.mult), reads=[psb[bank], cb["rows"]], writes=[ttb])
                P.op("dve", TT(xr[:, hc * 512:(hc + 1) * 512], xr[:, hc * 512:(hc + 1) * 512], tt[:], ALU.add), reads=[ttb, xrbf], writes=[xrbf])
            P.op("act", ACT(JUNKB[:], xr[:], AF.Square, accum_out=RSTD2[:, q:q + 1]), reads=[xrbf], writes=[jbb, r2b[q]])
            P.dma("sp", x1_q[q], xr[:], "c_x1st", reads=[xrbf], writes=[x1b[q]])
    P.op("act", ACT(RSTD2[:], RSTD2[:], AF.Sqrt, bias=EPS_AP[:, 0:1], scale=1.0 / D), reads=r2b + [cb["eps"]], writes=r2b)
    P.op("dve", RECIP(RSTD2[:], RSTD2[:]), reads=r2b, writes=r2b)
    P.barrier()

    go = DYN
    X1W = [P.sb(f"X1W{i}", [128, 6, D], F32, off=go + i * 24576) for i in range(2)]
    x1wb = [[Buf(f"x1w{i}_{j}") for j in range(6)] for i in range(2)]
    go2 = go + 49152
    H2W = [P.sb(f"H2W{i}", [128, 8, 768], BF16, off=go2 + i * 12288) for i in range(2)]
    h2wb = [[Buf(f"h2w{i}_{j}") for j in range(6)] for i in range(2)]
    go3 = go2 + 24576
    UG = [P.sb(f"UG{i}", [128, 640], BF16, off=go3 + i * 1280) for i in range(4)]
    ugb = [Buf(f"ug{i}") for i in range(4)]
    go4 = go3 + 5120
    PB = P.sb("PB", [128, 22, 512], BF16, off=go4)
    pbb = [Buf(f"pb{c}") for c in range(22)]
    go5 = go4 + 22528
    DG = P.sb("DG", [128, 2, 18, 128], BF16, off=go5)
    dgb = [Buf(f"dg{i}") for i in range(2)]
    go6 = go5 + 9216
    GTG = [P.sb(f"GTG{i}", [128, 512], F32, off=go6 + i * 2048) for i in range(4)]
    gtgb = [Buf(f"gtg{i}") for i in range(4)]
    go7 = go6 + 8192
    XSG = [P.sb(f"XSG{i}", [128, D], F32, off=go7 + i * 4096) for i in range(2)]
    xsgb = [Buf(f"xsg{i}") for i in range(2)]
    go8 = go7 + 8192
    JUNKG = P.sb("JUNKG", [128, D], BF16, off=go8)
    jgb = Buf("junkg")
    go9 = go8 + 2048
    OUTT = P.sb("OUTT", [128, 4, D], F32, off=go9)
    otb = [Buf(f"ot{j}") for j in range(4)]
    ring_off = go9 + 16384
    NSG = min(12, (SB_END - ring_off) // 4096)
    assert NSG >= 6, NSG
    ringG = Ring(P, "rgn", NSG, ring_off, 4096)
    planG = []
    for blk in range(8):
        d_ = {"up": [], "dn": []}
        for cp in range(11):
            d_["up"].append((ringG.plan(wu_s[cp], 2048, [wb["wu"][cp]]),
                             ringG.plan(wu_s[11 + cp], 2048, [wb["wu"][11 + cp]])))
        for hc in range(2):
            d_["dn"].append([ringG.plan(wd_s[i], 2048, [wb["wd"][i]]) for i in range(11)])
        planG.append(d_)
    AG = NSG - 2
    x1_v6 = x1_s.rearrange("(q p) d -> p q d", p=128)
    out_v = out_d.rearrange("(b j p) d -> b p j d", p=128, j=4)
    out_tks = []
    def mk_prepG(blk):
        s_ = blk % 2
        q0 = blk * 4 - 1
        qa, qb = max(q0, 0), min(q0 + 6, 32)
        ja, jb_ = qa - q0, qb - q0

        def st(j):
            if j == 0:
                P.dma("sp", X1W[s_][:, ja:jb_, :], x1_v6[:, qa:qb, :], f"c_x1w{s_}", reads=x1b[qa:qb], writes=x1wb[s_][ja:jb_])
            if j < ja or j >= jb_:
                P.op("pool", MS(H2W[s_][:, :, j * 128:(j + 1) * 128], 0.0), writes=[h2wb[s_][j]])
                return None
            return norm_stats(X1W[s_][:, j, :], x1wb[s_][j], XSG, xsgb, JUNKG, jgb, rstd=RSTD2[:, q0 + j:q0 + j + 1], rbuf=r2b[q0 + j])

        def fin(j, hdl):
            if hdl is not None:
                norm_finish(hdl, S2, B2, lambda kc, j=j: H2W[s_][:, kc, j * 128:(j + 1) * 128], h2wb[s_][j], (6, 7))
        return Prep(6, st, fin)

    pg = mk_prepG(0)
    pg.flush()
    deferred = None
    for blk in range(8):
        s = blk % 2
        pl = planG[blk]
        pg = mk_prepG(blk + 1) if blk + 1 < 8 else None
        for cp in range(11):
            if pg is not None and cp >= 1:
                pg.step()
            kg, kv = pl["up"][cp]
            s_g, b_g = ringG.get(kg, AG)
            s_v, b_v = ringG.get(kv, AG)
            vg = s_g[:, 0:2048].rearrange("p (kc c) -> p kc c", c=256)
            vv = s_v[:, 0:2048].rearrange("p (kc c) -> p kc c", c=256)
            for sub in range(2):
                c = 2 * cp + sub
                ds = c % 2
                P.op("pool", TT(DG[:, ds], identb[:].unsqueeze(1).broadcast_to([128, 18, 128]),
                                fcw[:, c, :].unsqueeze(2).broadcast_to([128, 18, 128]), ALU.mult),
                     reads=[cb["ident"], cb["fcp"]], writes=[dgb[ds]])
                for (wi, wv, wbuf, b0) in ((0, vg, b_g, 0), (1, vv, b_v, 2)):
                    for half in range(2):
                        bank = b0 + half
                        for kc in range(8):
                            P.op("pe", MM(ps[bank][:, 0:320], wv[:, kc, sub * 128:sub * 128 + 128], H2W[s][:, kc, 64 + half * 320:64 + (half + 1) * 320], kc == 0, kc == 7),
                                 reads=[wbuf] + h2wb[s], writes=[psb[bank]], sig=(kc == 7))
                    u, ub_ = UG[wi * 2 + ds], ugb[wi * 2 + ds]
                    P.op("act", ACT(u[:, 0:320], ps[b0][:, 0:320], AF.Copy), reads=[psb[b0]], writes=[ub_])
                    P.op("dve", CP(u[:, 320:640], ps[b0 + 1][:, 0:320]), reads=[psb[b0 + 1]], writes=[ub_])
                for wi in range(2):
                    u, ub_ = UG[wi * 2 + ds], ugb[wi * 2 + ds]
                    uv = u[:].rearrange("p (r c) -> p r c", c=64)
                    bank = 4 + wi
                    pv = ps[bank][:].rearrange("p (r c) -> p r c", c=64)
                    taps = [(0, 0)] + [(dr, dc) for dr in (-1, 0, 1) for dc in (-1, 0, 1) if (dr, dc) != (0, 0)]
                    for i_, (dr, dc) in enumerate(taps):
                        tix = (dr + 1) * 3 + (dc + 1)
                        lhs = DG[:, ds, wi * 9 + tix, :]
                        if dc == 0:
                            o_ap, r_ap = pv[:, :, :], uv[:, 1 + dr:9 + dr, :]
                        elif dc == -1:
                            o_ap, r_ap = pv[:, :, 1:64], uv[:, 1 + dr:9 + dr, 0:63]
                        else:
                            o_ap, r_ap = pv[:, :, 0:63], uv[:, 1 + dr:9 + dr, 1:64]
                        P.op("pe", MM(o_ap, lhs, r_ap, i_ == 0, i_ == 8), reads=[dgb[ds], ub_], writes=[psb[bank]], sig=(i_ == 8))
                gt, gb = GTG[c % 4], gtgb[c % 4]
                P.op("act", ACT(gt[:], ps[4][:], AF.Gelu_apprx_tanh, bias=fcb[:, c:c + 1]), reads=[psb[4], cb["fcp"]], writes=[gb])
                P.op("dve", STT(PB[:, c, :], ps[5][:], fcb[:, 22 + c:23 + c], gt[:], ALU.add, ALU.mult), reads=[psb[5], gb, cb["fcp"]], writes=[pbb[c]])
            if deferred is not None and cp < len(deferred):
                deferred[cp]()
        if pg is not None:
            pg.flush()
        for hc in range(2):
            for c in range(22):
                sl, sb_ = ringG.get(pl["dn"][hc][c // 2], AG)
                wv = sl[:, 0:2048].rearrange("p (kc c) -> p kc c", c=1024)
                for q in range(4):
                    P.op("pe", MM(ps[4 * hc + q][:], PB[:, c, q * 128:(q + 1) * 128], wv[:, c % 2, hc * 512:(hc + 1) * 512], c == 0, c == 21),
                         reads=[pbb[c], sb_], writes=[psb[4 * hc + q]], sig=(q == 3))
            for q in range(4):
                tt, ttb = GTG[q], gtgb[q]
                xsl = X1W[s][:, q + 1, hc * 512:(hc + 1) * 512]
                P.op("dve", TT(tt[:], ps[4 * hc + q][:], G2ROW[:, hc * 512:(hc + 1) * 512], ALU.mult), reads=[psb[4 * hc + q], cb["rows"]], writes=[ttb])
                P.op("dve", TT(xsl, xsl, tt[:], ALU.add), reads=[ttb, x1wb[s][q + 1]], writes=[x1wb[s][q + 1]])
        i = nstate["i"]
        nstate["i"] += 4
        c0_ = (2 * i) % 64
        if c0_ + 8 > 64:
            c0_ = 0
        smb = smallb[(i % 32)]
        scq = small[:, c0_:c0_ + 4]

        def mk_final(s=s, blk=blk, scq=scq, smb=smb):
            def sq(q):
                P.op("act", ACT(JUNKG[:], X1W[s][:, q + 1, :], AF.Square, accum_out=scq[:, q:q + 1]), reads=[x1wb[s][q + 1]], writes=[jgb, smb])

            def rs():
                P.op("act", ACT(scq, scq, AF.Sqrt, bias=EPS_AP[:, 0:1], scale=1.0 / D), reads=[smb, cb["eps"]], writes=[smb])
                P.op("dve", RECIP(scq, scq), reads=[smb], writes=[smb])

            def st(q):
                P.op("dve", STT(OUTT[:, q, :], X1W[s][:, q + 1, :], scq[:, q:q + 1], FGROW[:], ALU.mult, ALU.mult),
                     reads=[x1wb[s][q + 1], smb, cb["rows"]], writes=[otb[q]])

            def store():
                out_tks.append(P.dma("sp", out_v[blk], OUTT[:], "c_out", reads=otb))
            return [lambda: (sq(0), sq(1), sq(2), sq(3), rs(), st(0), st(1), st(2), st(3), store())]

        deferred = mk_final()
        if blk == 7:
            for f_ in deferred:
                f_()
            deferred = None
    finals = [out_tks[-1]]
    if debug:
        finals += [Tk(k, P.cnt[k]) for k in ("c_dbg0", "c_dbg1", "c_dbg2", "c_dbg3", "c_dbg4", "c_x1st")]
    P.barrier()
    P.emit(final_waits=finals)
    return nc


_CONSTS = {}


def _consts():
    if _CONSTS:
        return _CONSTS
    bf = ml_dtypes.bfloat16
    n = np.arange(4096, dtype=np.int64)
    k = np.arange(2048, dtype=np.int64)
    ang = 2.0 * np.pi * ((n[:, None] * k[None, :]) % 4096).astype(np.float64) / 4096.0
    C = np.cos(ang).astype(np.float32)
    S = np.sin(ang).astype(np.float32)
    Cr = C.reshape(2, 16, 128, 8, 256)
    Sr = S.reshape(2, 16, 128, 8, 256)
    cs = np.concatenate([Cr, Sr], axis=-1).transpose(3, 0, 2, 1, 4)
    _CONSTS["cs"] = np.ascontiguousarray(cs).astype(bf)
    sgn = np.where(n % 2 == 0, 1.0, -1.0).astype(np.float32).reshape(32, 128).T
    _CONSTS["cs2"] = np.ascontiguousarray(np.stack([sgn, np.zeros_like(sgn)], axis=-1)).astype(bf)
    c = np.arange(128, dtype=np.int64)
    a2 = 2.0 * np.pi * ((c[:, None] * c[None, :]) % 128).astype(np.float64) / 128.0
    _CONSTS["cc"] = np.concatenate([np.cos(a2), -np.sin(a2), np.sin(a2)], axis=1).astype(np.float32).astype(bf)
    _CONSTS["identf"] = np.eye(128, dtype=np.float32)
    _CONSTS["identb"] = np.eye(128, dtype=np.float32).astype(bf)
    return _CONSTS


def make_in_maps(x, c, ctx, c_ctx, mod_w, mod_b, norm1_g, norm2_g, w_in, lru_conv_w, lru_conv_b,
                 lru_ga_w, lru_ga_b, lru_gx_w, lru_gx_b, lru_lambda, w_fourier, w_lru_out, w_o,
                 ffn_w_up, ffn_conv_w, ffn_conv_b, ffn_w_down, final_g):
    f = lambda a: np.ascontiguousarray(np.asarray(a, dtype=np.float32))
    K = _consts()
    col8 = lambda v: f(np.asarray(v).reshape(8, 128).T)
    p28 = lambda v: f(np.asarray(v).reshape(2, 8, 128).transpose(2, 0, 1))
    shared = {
        "mod_w": f(mod_w[0]),
        "mod_b2": f(np.tile(np.asarray(mod_b[0]).reshape(1, -1), (2, 1))),
        "n1g": col8(norm1_g[0]), "n2g": col8(norm2_g[0]),
        "final_g": f(np.asarray(final_g).reshape(1, D)),
        "w_in": f(w_in[0]),
        "lcw": f(np.asarray(lru_conv_w[0]).reshape(4, 8, 128).transpose(2, 1, 0)),
        "lcb": col8(lru_conv_b[0]),
        "ga": f(np.asarray(lru_ga_w[0]).transpose(2, 0, 1, 3).reshape(128, 2048)),
        "gx": f(np.asarray(lru_gx_w[0]).transpose(2, 0, 1, 3).reshape(128, 2048)),
        "gab": p28(lru_ga_b[0]), "gxb": p28(lru_gx_b[0]), "lam": p28(lru_lambda[0]),
        "w_f": f(w_fourier[0]), "w_r": f(w_lru_out[0]), "w_o": f(w_o[0]),
        "w_up": f(ffn_w_up[0]),
        "fcw": f(np.concatenate([np.asarray(ffn_conv_w[0]).reshape(9, 44, 128)[:, 0:22].transpose(2, 1, 0),
                                 np.asarray(ffn_conv_w[0]).reshape(9, 44, 128)[:, 22:44].transpose(2, 1, 0)], axis=2)),
        "fcb": f(np.asarray(ffn_conv_b[0]).reshape(44, 128).T),
        "w_down": f(ffn_w_down[0]),
        "identf": K["identf"], "identb": K["identb"], "cc": K["cc"], "cs": K["cs"], "cs2": K["cs2"],
    }
    x = np.asarray(x)
    ctx = np.asarray(ctx)
    c = np.asarray(c)
    cc_col = np.asarray(c_ctx).reshape(8, 128).T
    maps = []
    for b in range(8):
        m = dict(shared)
        m["x"] = f(x[b])
        m["ctx"] = f(ctx[b])
        m["cvec"] = f(np.stack([c[b].reshape(8, 128).T, cc_col], axis=-1))
        maps.append(m)
    return maps


_NC = {}


def kernel(**inputs):
    if "nc" not in _NC:
        _NC["nc"] = build_program(debug=False)
    in_maps = make_in_maps(**inputs)
    res = run_bass_kernel_spmd(_NC["nc"], in_maps, core_ids=list(range(8)))
    out = np.stack([np.asarray(r["out"]) for r in res.results], axis=0)
    return out.astype(np.float32)
```

```python
import numpy as np
import ml_dtypes
import concourse.bass as bass
import concourse.mybir as mybir
from concourse.bass_utils import run_bass_kernel_spmd

AF = mybir.ActivationFunctionType
ALU = mybir.AluOpType
F32 = mybir.dt.float32
BF16 = mybir.dt.bfloat16
SB_BASE = 16640
SB_END = 229344
NTOK = 4096
D = 1024
NCTX = 256
DFF = 2816
EPS = 1e-6


class Tk:
    __slots__ = ("key", "val")

    def __init__(self, key, val):
        self.key = key
        self.val = val


class Buf:
    __slots__ = ("name", "w", "r")

    def __init__(self, name):
        self.name = name
        self.w = None
        self.r = []


class Prog:
    ENGS = ("pe", "act", "dve", "pool", "sp")

    def __init__(self, nc):
        self.nc = nc
        self.q = {e: [] for e in self.ENGS}
        self.cnt = {e: 0 for e in self.ENGS}
        self.seen = {e: {} for e in self.ENGS}
        self.pend = {e: [] for e in self.ENGS}
        self.dma_keys = []
        self.sb_off = SB_BASE
        self.offs = {}

    def sb(self, name, shape, dtype, off=None):
        esz = 2 if dtype == BF16 else 4
        n = 1
        for s in shape[1:]:
            n *= s
        nbytes = (n * esz + 63) // 64 * 64
        if off is None:
            off = self.sb_off
            self.sb_off += nbytes
        assert off % 32 == 0 and off + nbytes <= SB_END, (name, off, nbytes)
        self.offs[name] = off
        return self.nc.alloc_sbuf_tensor_at(name, list(shape), dtype, offset=off)

    def _wait(self, eng, tk):
        if tk is None:
            return
        if eng == "pe" and tk.key == "pe":
            return
        if self.seen[eng].get(tk.key, 0) >= tk.val:
            return
        self.seen[eng][tk.key] = tk.val
        self.q[eng].append(("wait", tk.key, tk.val))

    def _deps(self, eng, reads, writes, extra):
        for b in reads:
            self._wait(eng, b.w)
        for b in writes:
            self._wait(eng, b.w)
            for t in b.r:
                self._wait(eng, t)
        for t in extra:
            self._wait(eng, t)

    def _commit(self, tk, reads, writes):
        for b in reads:
            b.r.append(tk)
            if len(b.r) > 16:
                last = {}
                for t in b.r:
                    if t.key not in last or last[t.key].val < t.val:
                        last[t.key] = t
                b.r = list(last.values())
        for b in writes:
            b.w = tk
            b.r = []

    def op(self, eng, fn, reads=(), writes=(), sig=True, extra=()):
        self._deps(eng, reads, writes, extra)
        if not sig:
            self.q[eng].append(("op", fn, None))
            self.pend[eng].append((tuple(reads), tuple(writes)))
            return None
        self.cnt[eng] += 1
        tk = Tk(eng, self.cnt[eng])
        self.q[eng].append(("op", fn, eng))
        for (r, w) in self.pend[eng]:
            self._commit(tk, r, w)
        self.pend[eng] = []
        self._commit(tk, reads, writes)
        return tk

    def dma(self, eng, out, in_, key, reads=(), writes=(), extra=()):
        self._deps(eng, reads, writes, extra)
        if key not in self.cnt:
            self.cnt[key] = 0
            self.dma_keys.append(key)
        self.cnt[key] += 16
        tk = Tk(key, self.cnt[key])
        self.q[eng].append(("dma", out, in_, key))
        self._commit(tk, reads, writes)
        return tk

    def barrier(self, exclude_prefix="cv_"):
        for e in self.ENGS:
            assert not self.pend[e]
        keys = list(self.ENGS) + [k for k in self.dma_keys if not k.startswith(exclude_prefix)]
        for e in self.ENGS:
            for k in keys:
                if self.cnt[k] > 0:
                    self._wait(e, Tk(k, self.cnt[k]))

    def emit(self, final_waits=()):
        nc = self.nc
        sems = {}
        for k in list(self.ENGS) + self.dma_keys:
            sems[k] = nc.alloc_semaphore(name="s_" + k)
        for tk in final_waits:
            self._wait("sp", tk)
        q = self.q

        def replay(e, items):
            for it in items:
                if it[0] == "wait":
                    e.wait_ge(sems[it[1]], it[2])
                elif it[0] == "op":
                    ins = it[1](e)
                    if it[2] is not None:
                        ins.then_inc(sems[it[2]], 1)
                else:
                    e.dma_start(out=it[1], in_=it[2]).then_inc(sems[it[3]], 16)

        with nc.Block() as block:
            @block.tensor
            def _(e):
                replay(e, q["pe"])

            @block.scalar
            def _(e):
                replay(e, q["act"])

            @block.vector
            def _(e):
                replay(e, q["dve"])

            @block.gpsimd
            def _(e):
                replay(e, q["pool"])

            @block.sync
            def _(e):
                replay(e, q["sp"])


def MM(out, lhsT, rhs, start, stop):
    return lambda e: e.matmul(out, lhsT=lhsT, rhs=rhs, start=start, stop=stop)


def TR(out, in_, ident):
    return lambda e: e.transpose(out, in_, ident)


def ACT(out, in_, func, bias=None, scale=None, accum_out=None):
    kw = {}
    if bias is not None:
        kw["bias"] = bias
    if scale is not None:
        kw["scale"] = scale
    if accum_out is not None:
        kw["accum_out"] = accum_out
    return lambda e: e.activation(out=out, in_=in_, func=func, **kw)


def TS(out, in0, s1, s2, op0, op1=None):
    if op1 is None:
        return lambda e: e.tensor_scalar(out=out, in0=in0, scalar1=s1, scalar2=None, op0=op0)
    return lambda e: e.tensor_scalar(out=out, in0=in0, scalar1=s1, scalar2=s2, op0=op0, op1=op1)


def TT(out, in0, in1, op):
    return lambda e: e.tensor_tensor(out=out, in0=in0, in1=in1, op=op)


def STT(out, in0, scalar, in1, op0, op1):
    return lambda e: e.scalar_tensor_tensor(out=out, in0=in0, scalar=scalar, in1=in1, op0=op0, op1=op1)


def CP(out, in_):
    return lambda e: e.tensor_copy(out=out, in_=in_)


def MS(ap, v):
    return lambda e: e.memset(ap, v)


def SCAN(out, d0, d1, init):
    return lambda e: e.tensor_tensor_scan(out=out, data0=d0, data1=d1, initial=init, op0=ALU.mult, op1=ALU.add)


def RECIP(out, in_):
    return lambda e: e.reciprocal(out=out, in_=in_)


class Ring:
    def __init__(self, P, name, nslots, off, slot_bytes):
        self.P = P
        self.n = nslots
        self.slots = [P.sb(f"{name}{i}", [128, slot_bytes // 2], BF16, off=off + i * slot_bytes) for i in range(nslots)]
        self.bufs = [Buf(f"{name}{i}") for i in range(nslots)]
        self.keys = [f"rg_{name}{i}" for i in range(nslots)]
        self.reqs = []
        self.issued = 0

    def plan(self, dram_ap, nelem, src_bufs=()):
        self.reqs.append((dram_ap, nelem, tuple(src_bufs)))
        return len(self.reqs) - 1

    def get(self, k, ahead):
        lim = min(len(self.reqs), k + ahead + 1)
        while self.issued < lim:
            i = self.issued
            ap, nelem, src = self.reqs[i]
            s = i % self.n
            self.P.dma("sp", self.slots[s][:, 0:nelem], ap, self.keys[s], reads=list(src), writes=[self.bufs[s]])
            self.issued += 1
        s = k % self.n
        return self.slots[s], self.bufs[s]


def build_program(debug=False):
    nc = bass.Bass("TRN2", target_bir_lowering=False)
    P = Prog(nc)

    def din(name, shape, dt=F32):
        return nc.dram_tensor(name, list(shape), dt, kind="ExternalInput").ap()

    x_d = din("x", [NTOK, D])
    ctx_d = din("ctx", [NCTX, D])
    cvec_d = din("cvec", [128, 8, 2])
    modw_d = din("mod_w", [D, 6 * D])
    modb_d = din("mod_b2", [2, 6 * D])
    n1g_d = din("n1g", [128, 8])
    n2g_d = din("n2g", [128, 8])
    fg_d = din("final_g", [1, D])
    win_d = din("w_in", [D, 4608])
    lcw_d = din("lcw", [128, 8, 4])
    lcb_d = din("lcb", [128, 8])
    ga_d = din("ga", [128, 2048])
    gx_d = din("gx", [128, 2048])
    gab_d = din("gab", [128, 2, 8])
    gxb_d = din("gxb", [128, 2, 8])
    lam_d = din("lam", [128, 2, 8])
    wf_d = din("w_f", [512, D])
    wr_d = din("w_r", [D, D])
    wo_d = din("w_o", [D, D])
    wu_d = din("w_up", [D, 2 * DFF])
    fcw_d = din("fcw", [128, 22, 18])
    fcb_d = din("fcb", [128, 44])
    wd_d = din("w_down", [DFF, D])
    idf_d = din("identf", [128, 128])
    idb_d = din("identb", [128, 128], BF16)
    cc_d = din("cc", [128, 384], BF16)
    cs_d = din("cs", [8, 2, 128, 16, 512], BF16)
    cs2_d = din("cs2", [128, 32, 2], BF16)
    out_d = nc.dram_tensor("out", [NTOK, D], F32, kind="ExternalOutput").ap()

    win_s = nc.dram_tensor("win_s", [18, 128, 2048], BF16, kind="Internal").ap()
    wf_s = nc.dram_tensor("wf_s", [4, 128, 1024], BF16, kind="Internal").ap()
    wr_s = nc.dram_tensor("wr_s", [4, 128, 2048], BF16, kind="Internal").ap()
    wo_s = nc.dram_tensor("wo_s", [4, 128, 2048], BF16, kind="Internal").ap()
    wu_s = nc.dram_tensor("wu_s", [22, 128, 2048], BF16, kind="Internal").ap()
    wd_s = nc.dram_tensor("wd_s", [11, 128, 2048], BF16, kind="Internal").ap()
    x1_s = nc.dram_tensor("x1_s", [NTOK, D], F32, kind="ExternalOutput" if debug else "Internal").ap()
    if debug:
        dbg_zx = nc.dram_tensor("dbg_zx", [128, 8, 4100], BF16, kind="ExternalOutput").ap()
        dbg_y = nc.dram_tensor("dbg_y", [128, 4, 4096], BF16, kind="ExternalOutput").ap()
        dbg_zx0 = nc.dram_tensor("dbg_zx0", [128, 8, 4100], BF16, kind="ExternalOutput").ap()
        dbg_misc = nc.dram_tensor("dbg_misc", [128, 256], F32, kind="ExternalOutput").ap()

    ps = [nc.alloc_psum_tensor(f"ps{i}", [128, 512], F32) for i in range(8)]
    psb = [Buf(f"ps{i}") for i in range(8)]

    identf = P.sb("identf", [128, 128], F32)
    identb = P.sb("identb", [128, 128], BF16)
    cc = P.sb("cc", [128, 384], BF16)
    modc = P.sb("modc", [128, 48, 2], F32)
    S1 = P.sb("S1", [128, 2, 8], F32)
    B1 = P.sb("B1", [128, 2, 8], F32)
    S2 = P.sb("S2", [128, 8], F32)
    B2 = P.sb("B2", [128, 8], F32)
    n1g = P.sb("n1g", [128, 8], F32)
    n2g = P.sb("n2g", [128, 8], F32)
    lcw = P.sb("lcw", [128, 8, 4], F32)
    lcb = P.sb("lcb", [128, 8], F32)
    gab = P.sb("gab", [128, 2, 8], F32)
    gxb = P.sb("gxb", [128, 2, 8], F32)
    cneg = P.sb("cneg", [128, 2, 8], F32)
    cnegh = P.sb("cnegh", [128, 2, 8], F32)
    H0 = P.sb("H0", [128, 2, 8], F32)
    fcw = P.sb("fcw", [128, 22, 18], F32)
    fcb = P.sb("fcb", [128, 44], F32)
    small = P.sb("small", [128, 64], F32)
    G1ROW = P.sb("G1ROW", [128, D], F32)
    G2ROW = P.sb("G2ROW", [128, D], F32)
    FGROW = P.sb("FGROW", [128, D], F32)
    GA = P.sb("GA", [128, 2048], BF16)
    GX = P.sb("GX", [128, 2048], BF16)
    SEL = P.sb("SEL", [2, 128], F32)
    EPS_AP = P.sb("EPS_AP", [128, 1], F32)
    RSTD1 = P.sb("RSTD1", [128, 32], F32, off=P.offs["SEL"])
    RSTD2 = P.sb("RSTD2", [128, 32], F32, off=P.offs["SEL"] + 128)
    ONE_AP = P.sb("ONE_AP", [128, 1], F32)
    DYN = (P.sb_off + 63) // 64 * 64
    ZX_OFF = DYN
    Y_OFF = ZX_OFF + 65664
    ZF_OFF = Y_OFF + 32768
    T_OFF = ZF_OFF + 32768
    assert SB_END - T_OFF >= 51200, (SB_END - T_OFF)
    FGT = P.sb("FGT", [2, D], F32, off=T_OFF + 24576 + 256)

    ZX = P.sb("ZX", [128, 8, 4100], BF16, off=ZX_OFF)
    Y = P.sb("Y", [128, 4, 4096], BF16, off=Y_OFF)
    ZF = P.sb("ZF", [128, 32, 512], BF16, off=ZF_OFF)

    cb = {k: Buf(k) for k in ["ident", "cc", "modc", "modp", "lrup", "lrup2", "H0", "fcp", "rows", "gates", "sel", "fgt", "eps"]}
    smallb = [Buf(f"small{i}") for i in range(32)]
    r1b = [Buf(f"r1_{i}") for i in range(32)]
    r2b = [Buf(f"r2_{i}") for i in range(32)]
    zxb = [[Buf(f"zx{m}_{b}") for b in range(8)] for m in range(8)]
    zfb = [Buf(f"zf{n}") for n in range(32)]
    yb = [[Buf(f"y{g}_{b}") for b in range(8)] for g in range(4)]

    wb = {}

    def conv_dma(name, dst, src, npan, cw, per=4, lo=0):
        bufs = wb.get(name, []) if lo > 0 else []
        for i in range(lo, npan, per):
            j = min(npan, i + per)
            b = Buf(f"{name}{i}")
            for m in range(i, j):
                tk = P.dma("pool", dst[m].rearrange("p (kc c) -> p kc c", c=cw), src[m], f"cv_{name}{i}")
            b.w = tk
            bufs += [b] * (j - i)
        wb[name] = bufs

    win_v = win_d.rearrange("(kc p) (m c) -> m p kc c", p=128, c=256)
    P.dma("sp", identf[:], idf_d, "c_idf", writes=[cb["ident"]])
    P.dma("sp", identb[:], idb_d, "c_idb", writes=[cb["ident"]])
    P.dma("sp", cc[:], cc_d, "c_cc", writes=[cb["cc"]])
    P.dma("sp", n1g[:], n1g_d, "c_n1g", writes=[cb["modp"]])
    P.dma("sp", n2g[:], n2g_d, "c_n2g", writes=[cb["modp"]])
    P.dma("sp", lcw[:], lcw_d, "c_lcw", writes=[cb["lrup"]])
    P.dma("sp", lcb[:], lcb_d, "c_lcb", writes=[cb["lrup"]])
    P.dma("sp", gab[:], gab_d, "c_gab", writes=[cb["lrup"]])
    P.dma("sp", gxb[:], gxb_d, "c_gxb", writes=[cb["lrup"]])
    P.dma("sp", cneg[:], lam_d, "c_lam", writes=[cb["lrup2"]])
    P.dma("sp", fcw[:], fcw_d, "c_fcw", writes=[cb["fcp"]])
    P.dma("sp", fcb[:], fcb_d, "c_fcb", writes=[cb["fcp"]])

    MW = P.sb("MW", [128, 8, 6144], BF16, off=ZX_OFF)
    MROW = P.sb("MROW", [2, 6144], F32, off=ZF_OFF)
    MBROW = P.sb("MBROW", [2, 6144], F32, off=T_OFF)
    CV = P.sb("CV", [128, 8, 2], F32, off=T_OFF + 24576)
    SCB = P.sb("SCB", [128, 8, 2], BF16, off=T_OFF + 24576 + 64)
    mwb = [Buf(f"mw{i}") for i in range(6)]
    b_mrow, b_mbrow, b_cv, b_scb = Buf("mrow"), Buf("mbrow"), Buf("cv"), Buf("scb")
    modw_v = modw_d.rearrange("(kc p) n -> p kc n", p=128)
    P.dma("sp", CV[:], cvec_d, "c_cv", writes=[b_cv])
    P.dma("sp", MBROW[:], modb_d, "c_mb", writes=[b_mbrow])
    for i in range(6):
        P.dma("pool", MW[:, :, i * 1024:(i + 1) * 1024], modw_v[:, :, i * 1024:(i + 1) * 1024], f"c_mw{i}", writes=[mwb[i]])
    P.dma("pool", GA[:], ga_d, "c_ga", writes=[cb["gates"]])
    P.dma("pool", GX[:], gx_d, "c_gx", writes=[cb["gates"]])
    conv_dma("win", win_s, win_v, 8, 256)

    P.op("act", ACT(SCB[:], CV[:], AF.Silu), reads=[b_cv], writes=[b_scb])
    for cbk in range(12):
        bank = cbk % 2
        for kc in range(8):
            P.op("pe", MM(ps[bank][0:2, :], SCB[:, kc, :], MW[:, kc, cbk * 512:(cbk + 1) * 512], kc == 0, kc == 7),
                 reads=[b_scb, mwb[cbk // 2]], writes=[psb[bank]], sig=(kc == 7))
        P.op("dve", TT(MROW[:, cbk * 512:(cbk + 1) * 512], ps[bank][0:2, :], MBROW[:, cbk * 512:(cbk + 1) * 512], ALU.add),
             reads=[psb[bank], b_mbrow], writes=[b_mrow])
    pst = ps[2][:, 0:96].rearrange("p (j v) -> p j v", v=2)
    for j in range(48):
        P.op("pe", TR(pst[:, j, :], MROW[0:2, j * 128:(j + 1) * 128], identf[0:2, 0:2]),
             reads=[b_mrow, cb["ident"]], writes=[psb[2]], sig=(j == 47))
    P.op("dve", CP(modc[:], pst), reads=[psb[2]], writes=[cb["modc"]])
    for v in range(2):
        P.op("dve", STT(S1[:, v, :], modc[:, 8:16, v], 1.0, n1g[:], ALU.add, ALU.mult), reads=[cb["modc"], cb["modp"]], writes=[cb["modp"]])
        P.op("dve", CP(B1[:, v, :], modc[:, 0:8, v]), reads=[cb["modc"]], writes=[cb["modp"]])
    P.op("dve", STT(S2[:], modc[:, 32:40, 0], 1.0, n2g[:], ALU.add, ALU.mult), reads=[cb["modc"], cb["modp"]], writes=[cb["modp"]])
    P.op("dve", CP(B2[:], modc[:, 24:32, 0]), reads=[cb["modc"]], writes=[cb["modp"]])
    P.op("dve", MS(SEL[:], 0.0), writes=[cb["sel"]])
    P.op("dve", MS(SEL[0:1, :], 1.0), writes=[cb["sel"]])
    P.op("dve", MS(FGT[:], 0.0), writes=[cb["fgt"]])
    P.dma("sp", FGT[0:1, :], fg_d, "c_fg", writes=[cb["fgt"]])
    for (row, src, c0, sb_) in ((G1ROW, MROW, 2048, b_mrow), (G2ROW, MROW, 5120, b_mrow), (FGROW, FGT, 0, cb["fgt"])):
        for hc in range(2):
            bank = 3 + hc
            P.op("pe", MM(ps[bank][:], SEL[:], src[:, c0 + hc * 512:c0 + (hc + 1) * 512], True, True),
                 reads=[cb["sel"], sb_], writes=[psb[bank]])
            P.op("act", ACT(row[:, hc * 512:(hc + 1) * 512], ps[bank][:], AF.Copy), reads=[psb[bank]], writes=[cb["rows"]])
    P.op("act", ACT(cneg[:], cneg[:], AF.Exp, scale=-1.0), reads=[cb["lrup2"]], writes=[cb["lrup2"]])
    P.op("act", ACT(cneg[:], cneg[:], AF.Ln, bias=1.0), reads=[cb["lrup2"]], writes=[cb["lrup2"]])
    P.op("dve", TS(cnegh[:], cneg[:], -4.0, None, ALU.mult), reads=[cb["lrup2"]], writes=[cb["lrup2"]])
    P.op("dve", TS(cneg[:], cneg[:], -8.0, None, ALU.mult), reads=[cb["lrup2"]], writes=[cb["lrup2"]])
    P.op("dve", TS(gab[:], gab[:], 0.5, None, ALU.mult), reads=[cb["lrup"]], writes=[cb["lrup"]])
    P.op("dve", TS(gxb[:], gxb[:], 0.5, None, ALU.mult), reads=[cb["lrup"]], writes=[cb["lrup"]])
    P.barrier()

    nstate = {"i": 0}

    def norm_stats(xt, xbuf, XS, xsb, JUNK, jb, rstd=None, rbuf=None, save=None, sbuf=None):
        i = nstate["i"]
        nstate["i"] += 1
        xs, xb_ = XS[i % len(XS)], xsb[i % len(XS)]
        if rstd is not None:
            P.op("dve", TS(xs[:], xt, rstd, None, ALU.mult), reads=[xbuf, rbuf], writes=[xb_])
            return (xs, xb_)
        sc = small[:, (2 * i) % 64:(2 * i) % 64 + 2]
        smb = smallb[i % 32]
        dst, dbuf = (save, sbuf) if save is not None else (sc[:, 1:2], smb)
        P.op("act", ACT(JUNK[:], xt, AF.Square, accum_out=sc[:, 0:1]), reads=[xbuf], writes=[jb, smb])
        P.op("act", ACT(dst, sc[:, 0:1], AF.Sqrt, bias=EPS_AP[:, 0:1], scale=1.0 / D), reads=[smb, cb["eps"]], writes=[dbuf])
        P.op("dve", RECIP(dst, dst), reads=[dbuf], writes=[dbuf])
        P.op("dve", TS(xs[:], xt, dst, None, ALU.mult), reads=[xbuf, dbuf], writes=[xb_])
        return (xs, xb_)

    def norm_finish(hdl, Scol, Bcol, out_fn, obuf, tr_banks):
        xs, xb_ = hdl
        for half in range(2):
            bank = tr_banks[half]
            for q in range(4):
                kc = half * 4 + q
                P.op("pe", TR(ps[bank][:, q * 128:(q + 1) * 128], xs[:, kc * 128:(kc + 1) * 128], identf[:]),
                     reads=[xb_, cb["ident"]], writes=[psb[bank]], sig=(q == 3))
            for q in range(4):
                kc = half * 4 + q
                if q % 2 == 0:
                    P.op("act", ACT(out_fn(kc), ps[bank][:, q * 128:(q + 1) * 128], AF.Identity, bias=Bcol[:, kc:kc + 1], scale=Scol[:, kc:kc + 1]),
                         reads=[psb[bank], cb["modp"]], writes=[obuf])
                else:
                    P.op("dve", TS(out_fn(kc), ps[bank][:, q * 128:(q + 1) * 128], Scol[:, kc:kc + 1], Bcol[:, kc:kc + 1], ALU.mult, ALU.add),
                         reads=[psb[bank], cb["modp"]], writes=[obuf])

    def norm_to_fm(xt, xbuf, Scol, Bcol, out_fn, obuf, XS, xsb, JUNK, jb, tr_banks):
        norm_finish(norm_stats(xt, xbuf, XS, xsb, JUNK, jb), Scol, Bcol, out_fn, obuf, tr_banks)

    class Prep:
        def __init__(self, n, stats_fn, finish_fn, lead=1):
            self.n, self.stats_fn, self.finish_fn, self.lead = n, stats_fn, finish_fn, lead
            self.j = 0
            self.q = []

        def step(self):
            if self.q and (len(self.q) >= self.lead or self.j >= self.n):
                jj, h = self.q.pop(0)
                if h is not None:
                    self.finish_fn(jj, h)
            if self.j < self.n:
                self.q.append((self.j, self.stats_fn(self.j)))
                self.j += 1

        def done(self):
            return self.j >= self.n and not self.q

        def flush(self):
            while not self.done():
                self.step()

    P.op("dve", MS(EPS_AP[:], EPS), writes=[cb["eps"]])

    P.op("dve", MS(ONE_AP[:], 0.25), writes=[cb["eps"]])

    XA = [P.sb(f"XA{i}", [128, 4, D], F32, off=Y_OFF + i * 16384) for i in range(2)]
    xab = [[Buf(f"xa{i}_{j}") for j in range(4)] for i in range(2)]
    WA = P.sb("WA", [128, 6, 8, 256], BF16, off=T_OFF)
    b_wa = Buf("wa")
    HL = [P.sb(f"HL{i}", [128, 8, 512], BF16, off=T_OFF + 24576 + i * 8192) for i in range(2)]
    hlb = [[Buf(f"hl{i}_{j}") for j in range(4)] for i in range(2)]
    JUNK = P.sb("JUNK", [128, D], BF16, off=T_OFF + 40960)
    jb = Buf("junk")
    XS = [P.sb(f"XS{i}", [128, D], F32, off=T_OFF + 43008 + i * 4096) for i in range(3)]
    xsb = [Buf(f"xs{i}") for i in range(3)]
    ZXC = P.sb("ZXC", [128, 8, 260], BF16, off=ZF_OFF)
    b_zxc = Buf("zxc")
    co = ZF_OFF + 4160
    cU = P.sb("cU", [128, 256], F32, off=co)
    cUB = P.sb("cUB", [128, 256], BF16, off=co + 1024)
    cHF = P.sb("cHF", [128, 256], F32, off=co + 1536)
    cR = P.sb("cR", [128, 256], F32, off=co + 2560)
    cI = P.sb("cI", [128, 256], F32, off=co + 3584)
    cA = P.sb("cA", [128, 256], F32, off=co + 4608)
    cHB = P.sb("cHB", [128, 256], F32, off=co + 5632)
    cCAR = P.sb("cCAR", [128, 16], F32, off=co + 6656)

    P.dma("sp", WA[:].rearrange("p m kc c -> p m (kc c)"), win_s[0:6].rearrange("m p f -> p m f"), "c_wa", reads=[wb["win"][0], wb["win"][4]], writes=[b_wa])
    P.op("dve", MS(ZXC[:], 0.0), writes=[b_zxc])

    P.dma("sp", XA[0][:, 0:2, :], ctx_d.rearrange("(j p) d -> p j d", p=128), "c_ctx", writes=[xab[0][0], xab[0][1]])
    for j in range(2):
        norm_to_fm(XA[0][:, j, :], xab[0][j], S1[:, 1, :], B1[:, 1, :],
                   lambda kc, j=j: HL[0][:, kc, j * 128:(j + 1) * 128], hlb[0][j], XS, xsb, JUNK, jb, (0, 1))
    for m in range(8):
        bank = 2 + m % 2
        for kc in range(8):
            P.op("pe", MM(ps[bank][:, 0:256], WA[:, 2 + m // 2, kc, (m % 2) * 128:(m % 2) * 128 + 128], HL[0][:, kc, 0:256], kc == 0, kc == 7),
                 reads=[b_wa, hlb[0][0], hlb[0][1]], writes=[psb[bank]], sig=(kc == 7))
        P.op("act", ACT(ZXC[:, m, 2:258], ps[bank][:, 0:256], AF.Copy), reads=[psb[bank]], writes=[b_zxc])
    zo = ZX_OFF
    kU = P.sb("kU", [128, 8, 256], F32, off=zo)
    kUB = P.sb("kUB", [128, 8, 256], BF16, off=zo + 8192)
    kR = P.sb("kR", [128, 8, 256], F32, off=zo + 12288)
    kI = P.sb("kI", [128, 8, 256], F32, off=zo + 20480)
    kA = P.sb("kA", [128, 8, 256], F32, off=zo + 28672)
    kH = P.sb("kH", [128, 8, 256], F32, off=zo + 36864)
    kub = [Buf(f"ku{h}") for h in range(8)]
    b_kub, b_kr, b_ki, b_ka, b_kh = Buf("kub"), Buf("kr"), Buf("ki"), Buf("ka"), Buf("kh")
    for k in range(4):
        for h in range(8):
            if k == 0:
                P.op("dve", TS(kU[:, h, :], ZXC[:, h, 0:256], lcw[:, h, 0:1], lcb[:, h:h + 1], ALU.mult, ALU.add),
                     reads=[b_zxc, cb["lrup"]], writes=[kub[h]])
            else:
                P.op("dve", STT(kU[:, h, :], ZXC[:, h, k:k + 256], lcw[:, h, k:k + 1], kU[:, h, :], ALU.mult, ALU.add),
                     reads=[b_zxc, cb["lrup"], kub[h]], writes=[kub[h]])
    P.op("pool", CP(kUB[:], kU[:]), reads=kub, writes=[b_kub])
    for d in range(2):
        for h in range(8):
            gsl = slice((d * 8 + h) * 128, (d * 8 + h + 1) * 128)
            bR, bI = h // 2, 4 + h // 2
            csl = slice((h % 2) * 256, (h % 2) * 256 + 256)
            P.op("pe", MM(ps[bR][:, csl], GA[:, gsl], kUB[:, h, :], True, True), reads=[cb["gates"], b_kub], writes=[psb[bR]])
            P.op("pe", MM(ps[bI][:, csl], GX[:, gsl], kUB[:, h, :], True, True), reads=[cb["gates"], b_kub], writes=[psb[bI]])
        for h in range(8):
            bR, bI = h // 2, 4 + h // 2
            csl = slice((h % 2) * 256, (h % 2) * 256 + 256)
            P.op("act", ACT(kR[:, h, :], ps[bR][:, csl], AF.Tanh, bias=gab[:, d, h:h + 1], scale=0.5), reads=[psb[bR], cb["lrup"]], writes=[b_kr])
            P.op("act", ACT(kI[:, h, :], ps[bI][:, csl], AF.Tanh, bias=gxb[:, d, h:h + 1], scale=0.5), reads=[psb[bI], cb["lrup"]], writes=[b_ki])
        for h in range(8):
            P.op("act", ACT(kA[:, h, :], kR[:, h, :], AF.Exp, bias=cnegh[:, d, h:h + 1], scale=cnegh[:, d, h:h + 1]), reads=[b_kr, cb["lrup2"]], writes=[b_ka])
        for h in range(8):
            P.op("act", ACT(kR[:, h, :], kR[:, h, :], AF.Exp, bias=cneg[:, d, h:h + 1], scale=cneg[:, d, h:h + 1]), reads=[b_kr, cb["lrup2"]], writes=[b_kr])
        P.op("act", ACT(kR[:], kR[:], AF.Sqrt, bias=ONE_AP[:, 0:1], scale=-0.25), reads=[b_kr, cb["eps"]], writes=[b_kr])
        P.op("dve", STT(kI[:], kI[:], 1.0, kR[:], ALU.add, ALU.mult), reads=[b_ki, b_kr], writes=[b_ki])
        P.op("dve", TT(kI[:], kI[:], kU[:], ALU.mult), reads=[b_ki] + kub, writes=[b_ki])
        for h in range(8):
            if d == 0:
                P.op("dve", SCAN(kH[:, h, :], kA[:, h, :], kI[:, h, :], 0.0), reads=[b_ka, b_ki], writes=[b_kh])
            else:
                P.op("dve", SCAN(kH[:, h, :][:, ::-1], kA[:, h, :][:, ::-1], kI[:, h, :][:, ::-1], 0.0), reads=[b_ka, b_ki], writes=[b_kh])
        P.op("dve", CP(H0[:, d, :], kH[:, :, 255] if d == 0 else kH[:, :, 0]), reads=[b_kh], writes=[cb["H0"]])
    P.barrier()
    conv_dma("win", win_s, win_v, 18, 256, lo=8)
    conv_dma("wf", wf_s, wf_d.rearrange("(kc p) (m c) -> m p kc c", p=128, c=256), 4, 256)
    conv_dma("wr", wr_s, wr_d.rearrange("(kc p) (m c) -> m p kc c", p=128, c=256), 4, 256)
    conv_dma("wo", wo_s, wo_d.rearrange("(q kc p) n -> q p kc n", p=128, kc=2), 4, 1024)
    conv_dma("wu", wu_s, wu_d.rearrange("(kc p) (m c) -> m p kc c", p=128, c=256), 22, 256)
    conv_dma("wd", wd_s, wd_d.rearrange("(q kc p) n -> q p kc n", p=128, kc=2), 11, 1024)
    b_zxpad = Buf("zxpad")
    P.op("pool", MS(ZX[:, :, 0:2], 0.0), writes=[b_zxpad])
    P.op("pool", MS(ZX[:, :, 4098:4100], 0.0), writes=[b_zxpad])

    x_v = x_d.rearrange("(b j p) d -> b p j d", p=128, j=4)

    def mk_prepA(blk):
        s_ = blk % 2

        def st(j):
            if j == 0:
                P.dma("sp", XA[s_][:], x_v[blk], f"c_xa{s_}", writes=xab[s_])
            return norm_stats(XA[s_][:, j, :], xab[s_][j], XS, xsb, JUNK, jb, save=RSTD1[:, blk * 4 + j:blk * 4 + j + 1], sbuf=r1b[blk * 4 + j])

        def fin(j, hdl):
            norm_finish(hdl, S1[:, 0, :], B1[:, 0, :], lambda kc, j=j: HL[s_][:, kc, j * 128:(j + 1) * 128], hlb[s_][j], (0, 1))
        return Prep(4, st, fin, lead=2)

    pa = mk_prepA(0)
    pa.flush()
    for blk in range(8):
        s = blk % 2
        pa = mk_prepA(blk + 1) if blk + 1 < 8 else None
        for m in range(8):
            bank = 2 + m % 3
            for kc in range(8):
                P.op("pe", MM(ps[bank][:], WA[:, 2 + m // 2, kc, (m % 2) * 128:(m % 2) * 128 + 128], HL[s][:, kc, :], kc == 0, kc == 7),
                     reads=[b_wa] + hlb[s], writes=[psb[bank]], sig=(kc == 7))
            if m % 2 == 0:
                P.op("act", ACT(ZX[:, m, 2 + blk * 512:2 + (blk + 1) * 512], ps[bank][:], AF.Copy), reads=[psb[bank]], writes=[zxb[m][blk]])
            else:
                P.op("dve", CP(ZX[:, m, 2 + blk * 512:2 + (blk + 1) * 512], ps[bank][:]), reads=[psb[bank]], writes=[zxb[m][blk]])
            if pa is not None:
                pa.step()
        if pa is not None:
            pa.flush()
        for j in range(4):
            bank = 5 + j % 3
            for half in range(2):
                for kc in range(8):
                    P.op("pe", MM(ps[bank][:, half * 256:(half + 1) * 256], HL[s][:, kc, j * 128:(j + 1) * 128], WA[:, half, kc, :], kc == 0, kc == 7),
                         reads=[b_wa, hlb[s][j]], writes=[psb[bank]], sig=(kc == 7 and half == 1))
            n = blk * 4 + j
            if j % 2 == 0:
                P.op("dve", CP(ZF[:, n, :], ps[bank][:]), reads=[psb[bank]], writes=[zfb[n]])
            else:
                P.op("act", ACT(ZF[:, n, :], ps[bank][:], AF.Copy), reads=[psb[bank]], writes=[zfb[n]])
    P.barrier()
    if debug:
        P.dma("sp", dbg_zx0, ZX[:], "c_dbg0", reads=[zxb[m][b] for m in range(8) for b in range(8)])
        P.barrier()

    CSB = [P.sb(f"CSB{i}", [128, 16, 512], BF16, off=T_OFF + i * 16384) for i in range(3)]
    csb = [Buf(f"csb{i}") for i in range(3)]
    PQB = [P.sb(f"PQB{i}", [128, 512], BF16, off=T_OFF + 49152 + i * 1024) for i in range(2)]
    pqb = [Buf(f"pqb{i}") for i in range(2)]
    CS2 = P.sb("CS2", [128, 32, 2], BF16, off=T_OFF + 51200)
    b_cs2 = Buf("cs2")
    P.dma("sp", CS2[:], cs2_d, "c_cs2", writes=[b_cs2])
    YSCALE = float(1.0 / np.sqrt(4096.0 * 128.0))
    nreq = 16
    issued = 0

    def cs_fetch(upto):
        nonlocal issued
        while issued < min(nreq, upto + 1):
            i = issued
            P.dma("sp", CSB[i % 3][:], cs_d[i // 2, i % 2], f"c_cs{i % 3}", writes=[csb[i % 3]])
            issued += 1

    def ybufs(g, lo, hi):
        return [yb[g][b_] for b_ in range(lo // 512, hi // 512 + 1)]

    ev = 0
    for kb in range(8):
        k0 = kb * 256
        for hh in range(2):
            r = kb * 2 + hh
            cs_fetch(r + 2)
            sl = r % 3
            for g in range(4):
                for nci in range(16):
                    n = hh * 16 + nci
                    P.op("pe", MM(ps[g][:], ZF[:, n, g * 128:(g + 1) * 128], CSB[sl][:, nci, :], n == 0, n == 31),
                         reads=[zfb[n], csb[sl]], writes=[psb[g]], sig=(nci == 15))
        for g in range(4):
            pq, pb = PQB[ev % 2], pqb[ev % 2]
            ybank = 4 + 2 * (ev % 2)
            ev += 1
            P.op("act", ACT(pq[:], ps[g][:], AF.Copy), reads=[psb[g]], writes=[pb])
            P.op("pe", MM(ps[ybank][:, 0:256], cc[:, 0:128], pq[:, 0:256], True, False), reads=[cb["cc"], pb], writes=[psb[ybank]], sig=False)
            P.op("pe", MM(ps[ybank][:, 0:256], cc[:, 128:256], pq[:, 256:512], False, True), reads=[cb["cc"], pb], writes=[psb[ybank]])
            P.op("pe", MM(ps[ybank + 1][:, 0:256], cc[:, 0:128], pq[:, 0:256], True, False), reads=[cb["cc"], pb], writes=[psb[ybank + 1]], sig=False)
            P.op("pe", MM(ps[ybank + 1][:, 0:256], cc[:, 256:384], pq[:, 256:512], False, True), reads=[cb["cc"], pb], writes=[psb[ybank + 1]])
            P.op("dve", TS(Y[:, g, k0:k0 + 256], ps[ybank][:, 0:256], YSCALE, None, ALU.mult), reads=[psb[ybank]], writes=ybufs(g, k0, k0 + 255))
            s0 = 1 if kb == 0 else 0
            lo_, hi_ = 4096 - k0 - 255, 4096 - k0 - s0
            P.op("act", ACT(Y[:, g, lo_:hi_ + 1][:, ::-1], ps[ybank + 1][:, s0:256], AF.Copy, scale=YSCALE), reads=[psb[ybank + 1]], writes=ybufs(g, lo_, hi_))
    for g in range(4):
        for n in range(32):
            P.op("pe", MM(ps[g][:, 0:2], ZF[:, n, g * 128:(g + 1) * 128], CS2[:, n, :], n == 0, n == 31),
                 reads=[zfb[n], b_cs2], writes=[psb[g]], sig=(n == 31))
    for g in range(4):
        pq, pb = PQB[ev % 2], pqb[ev % 2]
        ybank = 4 + 2 * (ev % 2)
        ev += 1
        P.op("act", ACT(pq[:, 0:2], ps[g][:, 0:2], AF.Copy), reads=[psb[g]], writes=[pb])
        P.op("pe", MM(ps[ybank][:, 0:2], cc[:, 0:128], pq[:, 0:2], True, True), reads=[cb["cc"], pb], writes=[psb[ybank]])
        P.op("dve", TS(Y[:, g, 2048:2049], ps[ybank][:, 0:1], YSCALE, None, ALU.mult), reads=[psb[ybank]], writes=[yb[g][4]])
    P.barrier()

    lo = ZF_OFF
    LU = P.sb("LU", [128, 4096], F32, off=lo)
    LUB = P.sb("LUB", [128, 4096], BF16, off=lo + 16384)
    LH1 = P.sb("LH1", [128, 4096], F32, off=lo + 24576)
    NSET = 3
    lsets = []
    for i in range(NSET):
        o_ = lo + 40960 + i * 12288
        lsets.append(dict(R=P.sb(f"LR{i}", [128, 1024], F32, off=o_), I=P.sb(f"LI{i}", [128, 1024], F32, off=o_ + 4096),
                          A=P.sb(f"LA{i}", [128, 1024], F32, off=o_ + 8192),
                          bR=Buf(f"lR{i}"), bI=Buf(f"lI{i}"), bA=Buf(f"lA{i}")))
    o_ = lo + 40960 + NSET * 12288
    LHB = P.sb("LHB", [128, 1024], F32, off=o_)
    LCAR = P.sb("LCAR", [128, 16], F32, off=o_ + 4096)
    LDG = P.sb("LDG", [128, 2, 4, 128], BF16, off=o_ + 4096 + 64)
    assert o_ + 4096 + 64 + 2048 <= SB_END
    ldgb = [Buf("ldg0"), Buf("ldg1")]
    b_lhb, b_lcar = Buf("lhb"), Buf("lcar")
    ub_ = [Buf(f"lu{i}") for i in range(4)]
    ubb_ = [Buf(f"lub{i}") for i in range(4)]
    h1b = [Buf(f"lh1{i}") for i in range(4)]
    TH, TW, NH = 1024, 512, 4

    def conv_seg(h, hf):
        ds = h % 2
        for ti in (2 * hf, 2 * hf + 1):
            bank = 4 + ti % 4
            for k in range(4):
                P.op("pe", MM(ps[bank][:], LDG[:, ds, k, :], ZX[:, h, ti * TW + k:ti * TW + k + TW], k == 0, k == 3),
                     reads=[ldgb[ds], b_zxpad] + zxb[h], writes=[psb[bank]], sig=(k == 3))
            P.op("act", ACT(LU[:, ti * TW:(ti + 1) * TW], ps[bank][:], AF.Identity, bias=lcb[:, h:h + 1]),
                 reads=[psb[bank], cb["lrup"]], writes=[ub_[hf]])
        P.op("pool", CP(LUB[:, hf * TH:(hf + 1) * TH], LU[:, hf * TH:(hf + 1) * TH]), reads=[ub_[hf]], writes=[ubb_[hf]])

    def diag_build(h):
        P.op("pool", TT(LDG[:, h % 2], identb[:].unsqueeze(1).broadcast_to([128, 4, 128]),
                        lcw[:, h, :].unsqueeze(2).broadcast_to([128, 4, 128]), ALU.mult),
             reads=[cb["ident"], cb["lrup"]], writes=[ldgb[h % 2]])

    diag_build(0)
    for hf in range(4):
        conv_seg(0, hf)
    gseg = 0
    for h in range(8):
        if h + 1 < 8:
            diag_build(h + 1)
        dirs = (0, 1) if h % 2 == 0 else (1, 0)
        segs = []
        for di, d in enumerate(dirs):
            order = list(range(NH)) if d == 0 else list(range(NH - 1, -1, -1))
            for oi, hf in enumerate(order):
                segs.append((di, d, oi, hf))
        for pair in range(0, 8, 2):
            info = []
            for s_ in (pair, pair + 1):
                di, d, oi, hf = segs[s_]
                S_ = lsets[gseg % NSET]
                gseg += 1
                info.append((s_, di, d, oi, hf, S_))
                R, I_, A = S_["R"], S_["I"], S_["A"]
                t0 = hf * TH
                gsl = slice((d * 8 + h) * 128, (d * 8 + h + 1) * 128)
                bR, bI = 2 * (s_ % 2), 2 * (s_ % 2) + 1
                for ti in range(TH // TW):
                    c0 = t0 + ti * TW
                    P.op("pe", MM(ps[bR][:], GA[:, gsl], LUB[:, c0:c0 + TW], True, True), reads=[cb["gates"], ubb_[hf]], writes=[psb[bR]])
                    P.op("pe", MM(ps[bI][:], GX[:, gsl], LUB[:, c0:c0 + TW], True, True), reads=[cb["gates"], ubb_[hf]], writes=[psb[bI]])
                    P.op("act", ACT(R[:, ti * TW:(ti + 1) * TW], ps[bR][:], AF.Tanh, bias=gab[:, d, h:h + 1], scale=0.5),
                         reads=[psb[bR], cb["lrup"]], writes=[S_["bR"]])
                    P.op("act", ACT(I_[:, ti * TW:(ti + 1) * TW], ps[bI][:], AF.Tanh, bias=gxb[:, d, h:h + 1], scale=0.5),
                         reads=[psb[bI], cb["lrup"]], writes=[S_["bI"]])
                P.op("act", ACT(A[:], R[:], AF.Exp, bias=cnegh[:, d, h:h + 1], scale=cnegh[:, d, h:h + 1]), reads=[S_["bR"], cb["lrup2"]], writes=[S_["bA"]])
                P.op("dve", TT(R[:], A[:], A[:], ALU.mult), reads=[S_["bA"], S_["bR"]], writes=[S_["bR"]])
            for (s_, di, d, oi, hf, S_) in info:
                P.op("act", ACT(S_["R"][:], S_["R"][:], AF.Sqrt, bias=ONE_AP[:, 0:1], scale=-0.25), reads=[S_["bR"], cb["eps"]], writes=[S_["bR"]])
            for (s_, di, d, oi, hf, S_) in info:
                R, I_, A = S_["R"], S_["I"], S_["A"]
                t0 = hf * TH
                P.op("dve", STT(I_[:], I_[:], 1.0, R[:], ALU.add, ALU.mult), reads=[S_["bI"], S_["bR"]], writes=[S_["bI"]])
                P.op("dve", TT(I_[:], I_[:], LU[:, t0:t0 + TH], ALU.mult), reads=[S_["bI"], ub_[hf]], writes=[S_["bI"]])
                rev = (lambda ap: ap[:, ::-1]) if d == 1 else (lambda ap: ap)
                if di == 0:
                    if oi == 0:
                        init = H0[:, d, h:h + 1]
                    else:
                        init = LH1[:, t0 - 1:t0] if d == 0 else LH1[:, t0 + TH:t0 + TH + 1]
                    prev = [] if oi == 0 else [h1b[hf - 1 if d == 0 else hf + 1]]
                    P.op("dve", SCAN(rev(LH1[:, t0:t0 + TH]), rev(A[:]), rev(I_[:]), init),
                         reads=[S_["bA"], S_["bI"], cb["H0"]] + prev, writes=[h1b[hf]])
                else:
                    init = H0[:, d, h:h + 1] if oi == 0 else LCAR[:, 0:1]
                    P.op("dve", SCAN(rev(LHB[:]), rev(A[:]), rev(I_[:]), init),
                         reads=[S_["bA"], S_["bI"], cb["H0"], b_lcar], writes=[b_lhb])
                    if oi < NH - 1:
                        edge = LHB[:, TH - 1:TH] if d == 0 else LHB[:, 0:1]
                        P.op("dve", CP(LCAR[:, 0:1], edge), reads=[b_lhb], writes=[b_lcar])
                    P.op("dve", TT(ZX[:, h, 2 + t0:2 + t0 + TH], LH1[:, t0:t0 + TH], LHB[:], ALU.add),
                         reads=[h1b[hf], b_lhb], writes=[zxb[h][2 * hf], zxb[h][2 * hf + 1]])
                    if h + 1 < 8:
                        conv_seg(h + 1, hf)
    P.barrier()
    if debug:
        P.dma("sp", dbg_zx, ZX[:], "c_dbg1", reads=[zxb[m][b] for m in range(8) for b in range(8)])
        P.dma("sp", dbg_y, Y[:], "c_dbg2", reads=[yb[g][b] for g in range(4) for b in range(8)])
        P.dma("sp", dbg_misc[:, 0:96], modc[:].rearrange("p j v -> p (j v)"), "c_dbg3", reads=[cb["modc"]])
        P.dma("sp", dbg_misc[:, 96:112], H0[:].rearrange("p d h -> p (d h)"), "c_dbg4", reads=[cb["H0"]])
        P.barrier()

    bo = ZF_OFF
    XQ = [P.sb(f"XQ{i}", [128, D], F32, off=bo + i * 4096) for i in range(6)]
    xqb = [Buf(f"xq{i}") for i in range(6)]
    xq_state = {"i": 0}

    def xq_load(q):
        i = xq_state["i"] % 6
        xq_state["i"] += 1
        P.dma("sp", XQ[i][:], x_q[q], f"c_xq{i}", writes=[xqb[i]])
        return XQ[i], xqb[i]
    HLB = [P.sb(f"HLB{i}", [128, 8, 512], BF16, off=bo + 24576 + i * 8192) for i in range(2)]
    hlbb = [[Buf(f"hlb{i}_{j}") for j in range(4)] for i in range(2)]
    MG = P.sb("MG", [128, 8, 512], BF16, off=bo + 40960)
    mgb = [Buf(f"mg{m}") for m in range(8)]
    SG = [P.sb(f"SG{i}", [128, 512], F32, off=P.offs["GA"] + i * 2048) for i in range(4)]
    sgb = [Buf(f"sg{i}") for i in range(4)]
    GT = [P.sb(f"GT{i}", [128, 512], F32, off=bo + 49152 + i * 2048) for i in range(2)]
    gtb = [Buf(f"gt{i}") for i in range(2)]
    XSB = [P.sb(f"XSB{i}", [128, D], F32, off=bo + 53248 + i * 4096) for i in range(2)]
    xsbb = [Buf(f"xsb{i}") for i in range(2)]
    JUNKB = P.sb("JUNKB", [128, D], BF16, off=bo + 61440)
    jbb = Buf("junkb")
    ring_off = bo + 63488
    NSB = min(8, (SB_END - ring_off) // 4096)
    assert NSB >= 6, NSB
    ringB = Ring(P, "rb", NSB, ring_off, 4096)
    planB = []
    for blk in range(8):
        d_ = {}
        d_["y"] = [ringB.plan(win_s[6 + i], 2048, [wb["win"][6 + i]]) for i in range(4)]
        d_["mp"] = []
        for mp in range(4):
            d_["mp"].append((
                ringB.plan(win_s[10 + mp], 2048, [wb["win"][10 + mp]]),
                ringB.plan(win_s[14 + mp], 2048, [wb["win"][14 + mp]]),
                ringB.plan(wf_s[mp], 1024, [wb["wf"][mp]]),
                ringB.plan(wr_s[mp], 2048, [wb["wr"][mp]])))
        d_["o"] = [ringB.plan(wo_s[i], 2048, [wb["wo"][i]]) for i in range(4)]
        planB.append(d_)
    AH = NSB - 4
    x1b = [Buf(f"x1s{q}") for q in range(32)]
    x_q = x_d.rearrange("(q p) d -> q p d", p=128)
    x1_q = x1_s.rearrange("(q p) d -> q p d", p=128)

    def mk_prepB(blk):
        def st(j):
            q = blk * 4 + j
            xt_, xb__ = xq_load(q)
            return norm_stats(xt_[:], xb__, XSB, xsbb, JUNKB, jbb, rstd=RSTD1[:, q:q + 1], rbuf=r1b[q])

        def fin(j, hdl):
            norm_finish(hdl, S1[:, 0, :], B1[:, 0, :], lambda kc, j=j: HLB[blk % 2][:, kc, j * 128:(j + 1) * 128], hlbb[blk % 2][j], (0, 1))
        return Prep(4, st, fin)

    pb_ = mk_prepB(0)
    pb_.flush()
    for blk in range(8):
        t0 = blk * 512
        hl, hlbf = HLB[blk % 2], hlbb[blk % 2]
        pl = planB[blk]
        pb_ = mk_prepB(blk + 1) if blk + 1 < 8 else None
        for m in range(8):
            slot, sbuf_ = ringB.get(pl["y"][m // 2], AH)
            wv = slot[:, 0:2048].rearrange("p (kc c) -> p kc c", c=256)
            bank = 2 + m % 2
            for kc in range(8):
                P.op("pe", MM(ps[bank][:], wv[:, kc, (m % 2) * 128:(m % 2) * 128 + 128], hl[:, kc, :], kc == 0, kc == 7),
                     reads=[sbuf_] + hlbf, writes=[psb[bank]], sig=(kc == 7))
            gt, gb = GT[m % 2], gtb[m % 2]
            P.op("act", ACT(gt[:], ps[bank][:], AF.Gelu_apprx_tanh), reads=[psb[bank]], writes=[gb])
            zsl = ZX[:, m, 2 + t0:2 + t0 + 512]
            P.op("dve", TT(zsl, gt[:], zsl, ALU.mult), reads=[gb, zxb[m][blk]], writes=[zxb[m][blk]])
        for m in range(8):
            mp, sub = m // 2, (m % 2) * 128
            kgf, kgr, kwf, kwr = pl["mp"][mp]
            s_gf, b_gf = ringB.get(kgf, AH)
            s_gr, b_gr = ringB.get(kgr, AH)
            s_wf, b_wf = ringB.get(kwf, AH)
            s_wr, b_wr = ringB.get(kwr, AH)
            v_gf = s_gf[:, 0:2048].rearrange("p (kc c) -> p kc c", c=256)
            v_gr = s_gr[:, 0:2048].rearrange("p (kc c) -> p kc c", c=256)
            v_wf = s_wf[:, 0:1024].rearrange("p (kc c) -> p kc c", c=256)
            v_wr = s_wr[:, 0:2048].rearrange("p (kc c) -> p kc c", c=256)
            for kc in range(8):
                P.op("pe", MM(ps[4][:], v_gf[:, kc, sub:sub + 128], hl[:, kc, :], kc == 0, kc == 7), reads=[b_gf] + hlbf, writes=[psb[4]], sig=(kc == 7))
            for kc in range(8):
                P.op("pe", MM(ps[5][:], v_gr[:, kc, sub:sub + 128], hl[:, kc, :], kc == 0, kc == 7), reads=[b_gr] + hlbf, writes=[psb[5]], sig=(kc == 7))
            for kc in range(4):
                P.op("pe", MM(ps[6][:], v_wf[:, kc, sub:sub + 128], Y[:, kc, t0:t0 + 512], kc == 0, kc == 3), reads=[b_wf, yb[kc][blk]], writes=[psb[6]], sig=(kc == 3))
            for kc in range(8):
                P.op("pe", MM(ps[7][:], v_wr[:, kc, sub:sub + 128], ZX[:, kc, 2 + t0:2 + t0 + 512], kc == 0, kc == 7), reads=[b_wr, zxb[kc][blk]], writes=[psb[7]], sig=(kc == 7))
            sa, sab = SG[(2 * m) % 4], sgb[(2 * m) % 4]
            sr, srb = SG[(2 * m + 1) % 4], sgb[(2 * m + 1) % 4]
            P.op("act", ACT(sa[:], ps[4][:], AF.Sigmoid), reads=[psb[4]], writes=[sab])
            P.op("act", ACT(sr[:], ps[5][:], AF.Sigmoid), reads=[psb[5]], writes=[srb])
            P.op("dve", TT(sa[:], sa[:], ps[6][:], ALU.mult), reads=[sab, psb[6]], writes=[sab])
            P.op("dve", TT(sr[:], sr[:], ps[7][:], ALU.mult), reads=[srb, psb[7]], writes=[srb])
            P.op("dve", TT(MG[:, m, :], sa[:], sr[:], ALU.add), reads=[sab, srb], writes=[mgb[m]])
            if pb_ is not None and m >= 2:
                pb_.step()
        if pb_ is not None:
            pb_.flush()
        so = [ringB.get(pl["o"][i], AH) for i in range(4)]
        ei = 0
        xrs = [xq_load(blk * 4 + j) for j in range(4)]
        for j in range(4):
            q = blk * 4 + j
            xr, xrbf = xrs[j]
            for hc in range(2):
                bank = 2 + ei % 2
                for kc in range(8):
                    wv = so[kc // 2][0][:, 0:2048].rearrange("p (kc c) -> p kc c", c=1024)
                    P.op("pe", MM(ps[bank][:], MG[:, kc, j * 128:(j + 1) * 128], wv[:, kc % 2, hc * 512:(hc + 1) * 512], kc == 0, kc == 7),
                         reads=[mgb[kc], so[kc // 2][1]], writes=[psb[bank]], sig=(kc == 7))
                tt, ttb = GT[ei % 2], gtb[ei % 2]
                ei += 1
                P.op("dve", TT(tt[:], ps[bank][:], G1ROW[:, hc * 512:(hc + 1) * 512], ALU.mult), reads=[psb[bank], cb["rows"]], writes=[ttb])
                P.op("dve", TT(xr[:, hc * 512:(hc + 1) * 512], xr[:, hc * 512:(hc + 1) * 512], tt[:], ALU.add), reads=[ttb, xrbf], writes=[xrbf])
            P.op("act", ACT(JUNKB[:], xr[:], AF.Square, accum_out=RSTD2[:, q:q + 1]), reads=[xrbf], writes=[jbb, r2b[q]])
            P.dma("sp", x1_q[q], xr[:], "c_x1st", reads=[xrbf], writes=[x1b[q]])
    P.op("act", ACT(RSTD2[:], RSTD2[:], AF.Sqrt, bias=EPS_AP[:, 0:1], scale=1.0 / D), reads=r2b + [cb["eps"]], writes=r2b)
    P.op("dve", RECIP(RSTD2[:], RSTD2[:]), reads=r2b, writes=r2b)
    P.barrier()

    go = DYN
    X1W = [P.sb(f"X1W{i}", [128, 6, D], F32, off=go + i * 24576) for i in range(2)]
    x1wb = [[Buf(f"x1w{i}_{j}") for j in range(6)] for i in range(2)]
    go2 = go + 49152
    H2W = [P.sb(f"H2W{i}", [128, 8, 768], BF16, off=go2 + i * 12288) for i in range(2)]
    h2wb = [[Buf(f"h2w{i}_{j}") for j in range(6)] for i in range(2)]
    go3 = go2 + 24576
    UG = [P.sb(f"UG{i}", [128, 640], BF16, off=go3 + i * 1280) for i in range(4)]
    ugb = [Buf(f"ug{i}") for i in range(4)]
    go4 = go3 + 5120
    PB = P.sb("PB", [128, 22, 512], BF16, off=go4)
    pbb = [Buf(f"pb{c}") for c in range(22)]
    go5 = go4 + 22528
    DG = P.sb("DG", [128, 2, 18, 128], BF16, off=go5)
    dgb = [Buf(f"dg{i}") for i in range(2)]
    go6 = go5 + 9216
    GTG = [P.sb(f"GTG{i}", [128, 512], F32, off=go6 + i * 2048) for i in range(4)]
    gtgb = [Buf(f"gtg{i}") for i in range(4)]
    go7 = go6 + 8192
    XSG = [P.sb(f"XSG{i}", [128, D], F32, off=go7 + i * 4096) for i in range(2)]
    xsgb = [Buf(f"xsg{i}") for i in range(2)]
    go8 = go7 + 8192
    JUNKG = P.sb("JUNKG", [128, D], BF16, off=go8)
    jgb = Buf("junkg")
    go9 = go8 + 2048
    OUTT = P.sb("OUTT", [128, 4, D], F32, off=go9)
    otb = [Buf(f"ot{j}") for j in range(4)]
    ring_off = go9 + 16384
    NSG = min(12, (SB_END - ring_off) // 4096)
    assert NSG >= 6, NSG
    ringG = Ring(P, "rgn", NSG, ring_off, 4096)
    planG = []
    for blk in range(8):
        d_ = {"up": [], "dn": []}
        for cp in range(11):
            d_["up"].append((ringG.plan(wu_s[cp], 2048, [wb["wu"][cp]]),
                             ringG.plan(wu_s[11 + cp], 2048, [wb["wu"][11 + cp]])))
        for hc in range(2):
            d_["dn"].append([ringG.plan(wd_s[i], 2048, [wb["wd"][i]]) for i in range(11)])
        planG.append(d_)
    AG = NSG - 2
    x1_v6 = x1_s.rearrange("(q p) d -> p q d", p=128)
    out_v = out_d.rearrange("(b j p) d -> b p j d", p=128, j=4)
    out_tks = []
    def mk_prepG(blk):
        s_ = blk % 2
        q0 = blk * 4 - 1
        qa, qb = max(q0, 0), min(q0 + 6, 32)
        ja, jb_ = qa - q0, qb - q0

        def st(j):
            if j == 0:
                P.dma("sp", X1W[s_][:, ja:jb_, :], x1_v6[:, qa:qb, :], f"c_x1w{s_}", reads=x1b[qa:qb], writes=x1wb[s_][ja:jb_])
            if j < ja or j >= jb_:
                P.op("pool", MS(H2W[s_][:, :, j * 128:(j + 1) * 128], 0.0), writes=[h2wb[s_][j]])
                return None
            return norm_stats(X1W[s_][:, j, :], x1wb[s_][j], XSG, xsgb, JUNKG, jgb, rstd=RSTD2[:, q0 + j:q0 + j + 1], rbuf=r2b[q0 + j])

        def fin(j, hdl):
            if hdl is not None:
                norm_finish(hdl, S2, B2, lambda kc, j=j: H2W[s_][:, kc, j * 128:(j + 1) * 128], h2wb[s_][j], (6, 7))
        return Prep(6, st, fin)

    pg = mk_prepG(0)
    pg.flush()
    for blk in range(8):
        s = blk % 2
        pl = planG[blk]
        pg = mk_prepG(blk + 1) if blk + 1 < 8 else None
        for cp in range(11):
            if pg is not None and cp >= 1:
                pg.step()
            kg, kv = pl["up"][cp]
            s_g, b_g = ringG.get(kg, AG)
            s_v, b_v = ringG.get(kv, AG)
            vg = s_g[:, 0:2048].rearrange("p (kc c) -> p kc c", c=256)
            vv = s_v[:, 0:2048].rearrange("p (kc c) -> p kc c", c=256)
            for sub in range(2):
                c = 2 * cp + sub
                ds = c % 2
                P.op("pool", TT(DG[:, ds], identb[:].unsqueeze(1).broadcast_to([128, 18, 128]),
                                fcw[:, c, :].unsqueeze(2).broadcast_to([128, 18, 128]), ALU.mult),
                     reads=[cb["ident"], cb["fcp"]], writes=[dgb[ds]])
                for (wi, wv, wbuf, b0) in ((0, vg, b_g, 0), (1, vv, b_v, 2)):
                    for half in range(2):
                        bank = b0 + half
                        for kc in range(8):
                            P.op("pe", MM(ps[bank][:, 0:320], wv[:, kc, sub * 128:sub * 128 + 128], H2W[s][:, kc, 64 + half * 320:64 + (half + 1) * 320], kc == 0, kc == 7),
                                 reads=[wbuf] + h2wb[s], writes=[psb[bank]], sig=(kc == 7))
                    u, ub_ = UG[wi * 2 + ds], ugb[wi * 2 + ds]
                    P.op("act", ACT(u[:, 0:320], ps[b0][:, 0:320], AF.Copy), reads=[psb[b0]], writes=[ub_])
                    P.op("dve", CP(u[:, 320:640], ps[b0 + 1][:, 0:320]), reads=[psb[b0 + 1]], writes=[ub_])
                for wi in range(2):
                    u, ub_ = UG[wi * 2 + ds], ugb[wi * 2 + ds]
                    uv = u[:].rearrange("p (r c) -> p r c", c=64)
                    bank = 4 + wi
                    pv = ps[bank][:].rearrange("p (r c) -> p r c", c=64)
                    taps = [(0, 0)] + [(dr, dc) for dr in (-1, 0, 1) for dc in (-1, 0, 1) if (dr, dc) != (0, 0)]
                    for i_, (dr, dc) in enumerate(taps):
                        tix = (dr + 1) * 3 + (dc + 1)
                        lhs = DG[:, ds, wi * 9 + tix, :]
                        if dc == 0:
                            o_ap, r_ap = pv[:, :, :], uv[:, 1 + dr:9 + dr, :]
                        elif dc == -1:
                            o_ap, r_ap = pv[:, :, 1:64], uv[:, 1 + dr:9 + dr, 0:63]
                        else:
                            o_ap, r_ap = pv[:, :, 0:63], uv[:, 1 + dr:9 + dr, 1:64]
                        P.op("pe", MM(o_ap, lhs, r_ap, i_ == 0, i_ == 8), reads=[dgb[ds], ub_], writes=[psb[bank]], sig=(i_ == 8))
                gt, gb = GTG[c % 4], gtgb[c % 4]
                P.op("act", ACT(gt[:], ps[4][:], AF.Gelu_apprx_tanh, bias=fcb[:, c:c + 1]), reads=[psb[4], cb["fcp"]], writes=[gb])
                P.op("dve", STT(PB[:, c, :], ps[5][:], fcb[:, 22 + c:23 + c], gt[:], ALU.add, ALU.mult), reads=[psb[5], gb, cb["fcp"]], writes=[pbb[c]])
        if pg is not None:
            pg.flush()
        for hc in range(2):
            for c in range(22):
                sl, sb_ = ringG.get(pl["dn"][hc][c // 2], AG)
                wv = sl[:, 0:2048].rearrange("p (kc c) -> p kc c", c=1024)
                for q in range(4):
                    P.op("pe", MM(ps[4 * hc + q][:], PB[:, c, q * 128:(q + 1) * 128], wv[:, c % 2, hc * 512:(hc + 1) * 512], c == 0, c == 21),
                         reads=[pbb[c], sb_], writes=[psb[4 * hc + q]], sig=(q == 3))
            for q in range(4):
                tt, ttb = GTG[q], gtgb[q]
                xsl = X1W[s][:, q + 1, hc * 512:(hc + 1) * 512]
                P.op("dve", TT(tt[:], ps[4 * hc + q][:], G2ROW[:, hc * 512:(hc + 1) * 512], ALU.mult), reads=[psb[4 * hc + q], cb["rows"]], writes=[ttb])
                P.op("dve", TT(xsl, xsl, tt[:], ALU.add), reads=[ttb, x1wb[s][q + 1]], writes=[x1wb[s][q + 1]])
        i = nstate["i"]
        nstate["i"] += 4
        c0_ = (2 * i) % 64
        if c0_ + 8 > 64:
            c0_ = 0
        smb = smallb[(i % 32)]
        scq = small[:, c0_:c0_ + 4]
        for q in range(4):
            xt = X1W[s][:, q + 1, :]
            P.op("act", ACT(JUNKG[:], xt, AF.Square, accum_out=scq[:, q:q + 1]), reads=[x1wb[s][q + 1]], writes=[jgb, smb])
        P.op("act", ACT(scq, scq, AF.Sqrt, bias=EPS_AP[:, 0:1], scale=1.0 / D), reads=[smb, cb["eps"]], writes=[smb])
        P.op("dve", RECIP(scq, scq), reads=[smb], writes=[smb])
        for q in range(4):
            xt = X1W[s][:, q + 1, :]
            P.op("dve", STT(OUTT[:, q, :], xt, scq[:, q:q + 1], FGROW[:], ALU.mult, ALU.mult), reads=[x1wb[s][q + 1], smb, cb["rows"]], writes=[otb[q]])
        out_tks.append(P.dma("sp", out_v[blk], OUTT[:], "c_out", reads=otb))
    finals = [out_tks[-1]]
    if debug:
        finals += [Tk(k, P.cnt[k]) for k in ("c_dbg0", "c_dbg1", "c_dbg2", "c_dbg3", "c_dbg4", "c_x1st")]
    P.barrier()
    P.emit(final_waits=finals)
    return nc


_CONSTS = {}


def _consts():
    if _CONSTS:
        return _CONSTS
    bf = ml_dtypes.bfloat16
    n = np.arange(4096, dtype=np.int64)
    k = np.arange(2048, dtype=np.int64)
    ang = 2.0 * np.pi * ((n[:, None] * k[None, :]) % 4096).astype(np.float64) / 4096.0
    C = np.cos(ang).astype(np.float32)
    S = np.sin(ang).astype(np.float32)
    Cr = C.reshape(2, 16, 128, 8, 256)
    Sr = S.reshape(2, 16, 128, 8, 256)
    cs = np.concatenate([Cr, Sr], axis=-1).transpose(3, 0, 2, 1, 4)
    _CONSTS["cs"] = np.ascontiguousarray(cs).astype(bf)
    sgn = np.where(n % 2 == 0, 1.0, -1.0).astype(np.float32).reshape(32, 128).T
    _CONSTS["cs2"] = np.ascontiguousarray(np.stack([sgn, np.zeros_like(sgn)], axis=-1)).astype(bf)
    c = np.arange(128, dtype=np.int64)
    a2 = 2.0 * np.pi * ((c[:, None] * c[None, :]) % 128).astype(np.float64) / 128.0
    _CONSTS["cc"] = np.concatenate([np.cos(a2), -np.sin(a2), np.sin(a2)], axis=1).astype(np.float32).astype(bf)
    _CONSTS["identf"] = np.eye(128, dtype=np.float32)
    _CONSTS["identb"] = np.eye(128, dtype=np.float32).astype(bf)
    return _CONSTS


def make_in_maps(x, c, ctx, c_ctx, mod_w, mod_b, norm1_g, norm2_g, w_in, lru_conv_w, lru_conv_b,
                 lru_ga_w, lru_ga_b, lru_gx_w, lru_gx_b, lru_lambda, w_fourier, w_lru_out, w_o,
                 ffn_w_up, ffn_conv_w, ffn_conv_b, ffn_w_down, final_g):
    f = lambda a: np.ascontiguousarray(np.asarray(a, dtype=np.float32))
    K = _consts()
    col8 = lambda v: f(np.asarray(v).reshape(8, 128).T)
    p28 = lambda v: f(np.asarray(v).reshape(2, 8, 128).transpose(2, 0, 1))
    shared = {
        "mod_w": f(mod_w[0]),
        "mod_b2": f(np.tile(np.asarray(mod_b[0]).reshape(1, -1), (2, 1))),
        "n1g": col8(norm1_g[0]), "n2g": col8(norm2_g[0]),
        "final_g": f(np.asarray(final_g).reshape(1, D)),
        "w_in": f(w_in[0]),
        "lcw": f(np.asarray(lru_conv_w[0]).reshape(4, 8, 128).transpose(2, 1, 0)),
        "lcb": col8(lru_conv_b[0]),
        "ga": f(np.asarray(lru_ga_w[0]).transpose(2, 0, 1, 3).reshape(128, 2048)),
        "gx": f(np.asarray(lru_gx_w[0]).transpose(2, 0, 1, 3).reshape(128, 2048)),
        "gab": p28(lru_ga_b[0]), "gxb": p28(lru_gx_b[0]), "lam": p28(lru_lambda[0]),
        "w_f": f(w_fourier[0]), "w_r": f(w_lru_out[0]), "w_o": f(w_o[0]),
        "w_up": f(ffn_w_up[0]),
        "fcw": f(np.concatenate([np.asarray(ffn_conv_w[0]).reshape(9, 44, 128)[:, 0:22].transpose(2, 1, 0),
                                 np.asarray(ffn_conv_w[0]).reshape(9, 44, 128)[:, 22:44].transpose(2, 1, 0)], axis=2)),
        "fcb": f(np.asarray(ffn_conv_b[0]).reshape(44, 128).T),
        "w_down": f(ffn_w_down[0]),
        "identf": K["identf"], "identb": K["identb"], "cc": K["cc"], "cs": K["cs"], "cs2": K["cs2"],
    }
    x = np.asarray(x)
    ctx = np.asarray(ctx)
    c = np.asarray(c)
    cc_col = np.asarray(c_ctx).reshape(8, 128).T
    maps = []
    for b in range(8):
        m = dict(shared)
        m["x"] = f(x[b])
        m["ctx"] = f(ctx[b])
        m["cvec"] = f(np.stack([c[b].reshape(8, 128).T, cc_col], axis=-1))
        maps.append(m)
    return maps


_NC = {}


def kernel(**inputs):
    if "nc" not in _NC:
        _NC["nc"] = build_program(debug=False)
    in_maps = make_in_maps(**inputs)
    res = run_bass_kernel_spmd(_NC["nc"], in_maps, core_ids=list(range(8)))
    out = np.stack([np.asarray(r["out"]) for r in res.results], axis=0)
    return out.astype(np.float32)
```
